# Optimizing a Trainium2 kernel written in Bass

```python
import jax, jax.numpy as jnp
from jax import lax
import numpy as np


D_MODEL = 2048
BATCH = 4
SEQ = 2048
DEPTH = 4

GRID_W = 64
CTX_LEN = 256
HEAD_DIM = 128
N_HEADS = D_MODEL // 2 // HEAD_DIM
N_KV_HEADS = N_HEADS // 4
GROUP = N_HEADS // N_KV_HEADS
WINDOW = 128
BLOCK = 128
ROPE_THETA = 10000.0
ROPE_AXIS = HEAD_DIM // 2
POOL_GROUPS = 4
POOL_CH = D_MODEL // 4 // POOL_GROUPS
POOL_WINDOWS = (2, 4, 8, 16)
SGU_GROUPS = 4
SGU_CH = D_MODEL // 4 // SGU_GROUPS
CHUNK = 128
Q_W = N_HEADS * HEAD_DIM
KV_W = N_KV_HEADS * HEAD_DIM
POOL_W = POOL_GROUPS * POOL_CH
SGU_W = SGU_GROUPS * SGU_CH
MIX_W = Q_W + POOL_W + SGU_W
IN_W = Q_W + 2 * KV_W + POOL_W + 2 * SGU_W
D_FF = -(-8 * D_MODEL // (3 * 256)) * 256
EPS = 1e-6

kernel_name = "hybrid_parallel_group_dit_block"


def rms_norm(x, g):
    xf = x.astype(jnp.float32)
    y = xf * lax.rsqrt(jnp.mean(xf * xf, axis=-1, keepdims=True) + EPS)
    return (y * g.astype(jnp.float32)).astype(x.dtype)


def layer_norm(x, g):
    xf = x.astype(jnp.float32)
    mu = jnp.mean(xf, axis=-1, keepdims=True)
    var = jnp.mean(jnp.square(xf - mu), axis=-1, keepdims=True)
    return ((xf - mu) * lax.rsqrt(var + EPS) * g.astype(jnp.float32)).astype(x.dtype)


def modulate(h, shift, scale):
    return h * (1 + scale) + shift


def axial_rope_tables(rows):
    inv_freq = ROPE_THETA ** (-jnp.arange(0, ROPE_AXIS, 2, dtype=jnp.float32) / ROPE_AXIS)
    row = jnp.repeat(jnp.arange(rows, dtype=jnp.float32), GRID_W)
    col = jnp.tile(jnp.arange(GRID_W, dtype=jnp.float32), rows)
    ang = jnp.stack([row[:, None] * inv_freq, col[:, None] * inv_freq], axis=1)
    return jnp.cos(ang), jnp.sin(ang)


def apply_rope(x, cos, sin):
    B, S, H, _ = x.shape
    xr = x.reshape(B, S, H, 2, 2, ROPE_AXIS // 2).astype(jnp.float32)
    x1, x2 = xr[..., 0, :], xr[..., 1, :]
    c = cos[None, :, None]
    s = sin[None, :, None]
    out = jnp.stack([x1 * c - x2 * s, x2 * c + x1 * s], axis=-2)
    return out.reshape(B, S, H, HEAD_DIM).astype(x.dtype)


def split_in(p):
    B, S = p.shape[:2]
    q = p[..., :Q_W].reshape(B, S, N_HEADS, HEAD_DIM)
    k = p[..., Q_W:Q_W + KV_W].reshape(B, S, N_KV_HEADS, HEAD_DIM)
    v = p[..., Q_W + KV_W:Q_W + 2 * KV_W].reshape(B, S, N_KV_HEADS, HEAD_DIM)
    o = Q_W + 2 * KV_W
    pool_in = p[..., o:o + POOL_W]
    u = p[..., o + POOL_W:o + POOL_W + SGU_W]
    z = p[..., o + POOL_W + SGU_W:]
    return q, k, v, pool_in, u, z


def sink_column(sink, shape):
    s = sink.astype(jnp.float32).reshape(N_KV_HEADS, GROUP)
    return jnp.broadcast_to(s[:, :, None, None], shape[:-1] + (1,))


def window_attention(q, k, v, kc, vc, sink):
    B, S = q.shape[:2]
    nb = S // BLOCK
    scale = HEAD_DIM ** -0.5
    qb = q.reshape(B, nb, BLOCK, N_KV_HEADS, GROUP, HEAD_DIM)
    pad = ((0, 0), (BLOCK, BLOCK), (0, 0), (0, 0))
    kp = jnp.pad(k, pad).reshape(B, nb + 2, BLOCK, N_KV_HEADS, HEAD_DIM)
    vp = jnp.pad(v, pad).reshape(B, nb + 2, BLOCK, N_KV_HEADS, HEAD_DIM)
    kb = jnp.concatenate([kp[:, :-2], kp[:, 1:-1], kp[:, 2:]], axis=2)
    vb = jnp.concatenate([vp[:, :-2], vp[:, 1:-1], vp[:, 2:]], axis=2)
    s_loc = jnp.einsum('bnqhgd,bnkhd->bnhgqk', qb, kb).astype(jnp.float32) * scale
    s_ctx = jnp.einsum('bnqhgd,bchd->bnhgqc', qb, kc).astype(jnp.float32) * scale
    qi = jnp.arange(BLOCK)
    kj = jnp.arange(3 * BLOCK)
    rel = kj[None, :] - BLOCK - qi[:, None]
    key_pos = jnp.arange(nb)[:, None] * BLOCK - BLOCK + kj[None, :]
    mask = (jnp.abs(rel) <= WINDOW)[None] & ((key_pos >= 0) & (key_pos < S))[:, None, :]
    s_loc = jnp.where(mask[None, :, None, None], s_loc, -jnp.inf)
    sink_l = jnp.broadcast_to(sink.astype(jnp.float32).reshape(N_KV_HEADS, GROUP)[None, None, :, :, None, None],
                              s_loc.shape[:-1] + (1,))
    p = jax.nn.softmax(jnp.concatenate([s_loc, s_ctx, sink_l], axis=-1), axis=-1)
    nk = 3 * BLOCK
    nc = kc.shape[1]
    p_loc = p[..., :nk].astype(v.dtype)
    p_ctx = p[..., nk:nk + nc].astype(v.dtype)
    out = (jnp.einsum('bnhgqk,bnkhd->bnqhgd', p_loc, vb)
           + jnp.einsum('bnhgqc,bchd->bnqhgd', p_ctx, vc))
    return out.reshape(B, S, Q_W)


def context_attention(q, k, v, sink):
    B, C = q.shape[:2]
    qg = q.reshape(B, C, N_KV_HEADS, GROUP, HEAD_DIM)
    s = jnp.einsum('bqhgd,bkhd->bhgqk', qg, k).astype(jnp.float32) * HEAD_DIM ** -0.5
    sink_c = jnp.broadcast_to(sink.astype(jnp.float32).reshape(N_KV_HEADS, GROUP)[None, :, :, None, None],
                              s.shape[:-1] + (1,))
    p = jax.nn.softmax(jnp.concatenate([s, sink_c], axis=-1), axis=-1)[..., :C].astype(v.dtype)
    out = jnp.einsum('bhgqk,bkhd->bqhgd', p, v)
    return out.reshape(B, C, Q_W)


def pool_mixer(p, w, ch_scale):
    B, S = p.shape[:2]
    pg = p.reshape(B, S, POOL_GROUPS, POOL_CH)
    pf = pg.astype(jnp.float32)
    cs = jnp.concatenate([jnp.zeros((B, 1, POOL_GROUPS, POOL_CH), jnp.float32),
                          jnp.cumsum(pf, axis=1)], axis=1)
    half = jnp.array(POOL_WINDOWS, dtype=jnp.int32) // 2
    t = jnp.arange(S, dtype=jnp.int32)[:, None]
    lo = jnp.clip(t - half[None, :], 0, S)
    hi = jnp.clip(t + half[None, :], 0, S)
    gidx = jnp.arange(POOL_GROUPS)[None, :]
    win_sum = cs[:, hi, gidx, :] - cs[:, lo, gidx, :]
    cnt = (hi - lo).astype(jnp.float32)[None, :, :, None]
    pooled = (win_sum / cnt - pf).astype(p.dtype)
    out = jnp.einsum('bsgc,gcd->bsgd', pooled, w) * ch_scale.reshape(POOL_GROUPS, POOL_CH)
    return out.reshape(B, S, POOL_W)


def sgu_mixer(u, z, norm_g, w_s, b_s):
    B, S = u.shape[:2]
    n = S // CHUNK
    u = jax.nn.gelu(u, approximate=False).reshape(B, n, CHUNK, SGU_GROUPS, SGU_CH)
    z = jax.nn.gelu(z, approximate=False).reshape(B, S, SGU_GROUPS, SGU_CH)
    z = layer_norm(z, norm_g).reshape(B, n, CHUNK, SGU_GROUPS, SGU_CH)
    mixed = jnp.einsum('gpq,bnqgc->bnpgc', w_s, z) + b_s.T[None, None, :, :, None]
    return (u * mixed).reshape(B, S, SGU_W)


def swiglu(h, w_gu, w_dn):
    g, up = jnp.split(h @ w_gu, 2, axis=-1)
    return (jax.nn.silu(g) * up) @ w_dn


def setup_inputs(seed: int = 0) -> dict:
    key = jax.random.key(seed)
    ks = jax.random.split(key, 19)
    f = jnp.float32
    nrm = lambda k, shape, s: jax.random.normal(k, shape, f) * s
    return {
        "x": nrm(ks[0], (BATCH, SEQ, D_MODEL), 1.0),
        "c": nrm(ks[1], (BATCH, D_MODEL), 1.0),
        "ctx": nrm(ks[2], (BATCH, CTX_LEN, D_MODEL), 1.0),
        "c_ctx": nrm(ks[3], (D_MODEL,), 1.0),
        "w_ada": nrm(ks[4], (DEPTH, D_MODEL, 6 * D_MODEL), 0.5 * D_MODEL ** -0.5),
        "b_ada": nrm(ks[5], (DEPTH, 6 * D_MODEL), 0.01),
        "norm_mix_g": 1.0 + nrm(ks[6], (DEPTH, D_MODEL), 0.02),
        "norm_ffn_g": 1.0 + nrm(ks[7], (DEPTH, D_MODEL), 0.02),
        "w_in": nrm(ks[8], (DEPTH, D_MODEL, IN_W), D_MODEL ** -0.5),
        "attn_sink": nrm(ks[9], (DEPTH, N_HEADS), 0.5),
        "pool_w": nrm(ks[10], (DEPTH, POOL_GROUPS, POOL_CH, POOL_CH), POOL_CH ** -0.5),
        "pool_scale": 1.0 + nrm(ks[11], (DEPTH, POOL_W), 0.02),
        "sgu_norm_g": 1.0 + nrm(ks[12], (DEPTH, SGU_GROUPS, SGU_CH), 0.02),
        "sgu_w": nrm(ks[13], (DEPTH, SGU_GROUPS, CHUNK, CHUNK), CHUNK ** -0.5),
        "sgu_b": 1.0 + nrm(ks[14], (DEPTH, SGU_GROUPS, CHUNK), 0.02),
        "w_out": nrm(ks[15], (DEPTH, MIX_W, D_MODEL), MIX_W ** -0.5),
        "w_gate_up": nrm(ks[16], (DEPTH, D_MODEL, 2 * D_FF), D_MODEL ** -0.5),
        "w_down": nrm(ks[17], (DEPTH, D_FF, D_MODEL), D_FF ** -0.5),
        "final_norm_g": 1.0 + nrm(ks[18], (D_MODEL,), 0.02),
    }


def reference(x, c, ctx, c_ctx, w_ada, b_ada, norm_mix_g, norm_ffn_g, w_in, attn_sink, pool_w, pool_scale,
              sgu_norm_g, sgu_w, sgu_b, w_out, w_gate_up, w_down, final_norm_g):
    B = x.shape[0]
    C = ctx.shape[1]
    rows = x.shape[1] // GRID_W
    cos, sin = axial_rope_tables(rows)
    silu_c = jax.nn.silu(c)
    silu_cc = jax.nn.silu(c_ctx)
    h_ctx = ctx
    for l in range(DEPTH):
        last = l == DEPTH - 1
        mx = jnp.split((silu_c @ w_ada[l] + b_ada[l])[:, None, :], 6, axis=-1)
        mc = jnp.split(silu_cc @ w_ada[l] + b_ada[l], 6, axis=-1)
        hc = modulate(rms_norm(h_ctx, norm_mix_g[l]), mc[0], mc[1])
        if last:
            kc, vc = jnp.split(hc @ w_in[l][:, Q_W:Q_W + 2 * KV_W], 2, axis=-1)
            kc = kc.reshape(B, C, N_KV_HEADS, HEAD_DIM)
            vc = vc.reshape(B, C, N_KV_HEADS, HEAD_DIM)
        else:
            qc, kc, vc, poolc, uc, zc = split_in(hc @ w_in[l])
        hx = modulate(rms_norm(x, norm_mix_g[l]), mx[0], mx[1])
        qx, kx, vx, poolx, ux, zx = split_in(hx @ w_in[l])
        qx = apply_rope(qx, cos, sin)
        kx = apply_rope(kx, cos, sin)
        mix_x = jnp.concatenate([
            window_attention(qx, kx, vx, kc, vc, attn_sink[l]),
            pool_mixer(poolx, pool_w[l], pool_scale[l]),
            sgu_mixer(ux, zx, sgu_norm_g[l], sgu_w[l], sgu_b[l]),
        ], axis=-1)
        x = x + mx[2] * (mix_x @ w_out[l])
        x = x + mx[5] * swiglu(modulate(rms_norm(x, norm_ffn_g[l]), mx[3], mx[4]), w_gate_up[l], w_down[l])
        if not last:
            mix_c = jnp.concatenate([
                context_attention(qc, kc, vc, attn_sink[l]),
                pool_mixer(poolc, pool_w[l], pool_scale[l]),
                sgu_mixer(uc, zc, sgu_norm_g[l], sgu_w[l], sgu_b[l]),
            ], axis=-1)
            h_ctx = h_ctx + mc[2] * (mix_c @ w_out[l])
            h_ctx = h_ctx + mc[5] * swiglu(modulate(rms_norm(h_ctx, norm_ffn_g[l]), mc[3], mc[4]),
                                           w_gate_up[l], w_down[l])
    return rms_norm(x, final_norm_g)
```

```python
import contextlib
import math

import ml_dtypes
import numpy as np

import concourse.bass as bass
import concourse.mybir as mybir
from concourse.bass_utils import run_bass_kernel_spmd

F32 = mybir.dt.float32
BF16 = mybir.dt.bfloat16
AF = mybir.ActivationFunctionType
ALU = mybir.AluOpType
AX = mybir.AxisListType

DEPTH = 4
D = 2048
NDC = 16
EPS = 1e-6
SCALE = 128 ** -0.5
SQD = math.sqrt(2048.0)
NEG = -30000.0
RING = 8
LF = 16 + 16 + 96 + 4 + 4 + 8 + 512
MAXB = 4
STOP_AT = None
STOP_N = 1


class Prog:
    ENG = ("pe", "act", "dve", "pool", "sp")

    def __init__(self):
        self.ops = {e: [] for e in self.ENG}
        self.cnt = {e: 0 for e in self.ENG}
        self.dcnt = {}
        self.seen = {e: {} for e in self.ENG}
        self.lastw = {}
        self.readers = {}
        self.enabled = True
        self.stop_at = None
        self.stop_n = 1

    def stage(self, name):
        if self.stop_at is not None and name == self.stop_at:
            self.stop_n -= 1
            if self.stop_n <= 0:
                self.enabled = False

    def _deps(self, eng, reads, writes):
        deps = {}

        def add(k, v):
            if deps.get(k, 0) < v:
                deps[k] = v

        for r in reads:
            w = self.lastw.get(r)
            if w:
                add(*w)
            if r[0] == "ps":
                for k, v in self.readers.get(r, {}).items():
                    if k != eng:
                        add(k, v)
        for r in writes:
            w = self.lastw.get(r)
            if w:
                add(*w)
            for k, v in self.readers.get(r, {}).items():
                add(k, v)
        out = []
        for k, v in deps.items():
            if k == "pe" and eng == "pe":
                continue
            if self.seen[eng].get(k, 0) >= v:
                continue
            self.seen[eng][k] = v
            out.append((k, v))
        return out

    def _commit(self, reads, writes, tag):
        k, v = tag
        for r in writes:
            self.lastw[r] = tag
            self.readers[r] = {}
        for r in reads:
            d = self.readers.setdefault(r, {})
            if d.get(k, 0) < v:
                d[k] = v

    def op(self, eng, reads, writes, fn):
        if not self.enabled:
            return
        waits = self._deps(eng, reads, writes)
        self.cnt[eng] += 1
        self.ops[eng].append((waits, fn, (eng, 1)))
        self._commit(reads, writes, (eng, self.cnt[eng]))

    def dma(self, qeng, semname, reads, writes, fn):
        if not self.enabled:
            return
        waits = self._deps(qeng, reads, writes)
        self.dcnt[semname] = self.dcnt.get(semname, 0) + 16
        self.ops[qeng].append((waits, fn, (semname, 16)))
        self._commit(reads, writes, (semname, self.dcnt[semname]))


def layer_plan(l, depth):
    last = l == depth - 1
    NL = 8 + (depth - 1 - l)
    lat = [("L", i) for i in range(NL)]
    ctx = [("C", 0), ("C", 1)]
    fc_all = lat if last else ctx + lat
    groups = []
    if last:
        groups.append(dict(fc=[], kv=list(ctx)))
    n = len(fc_all)
    ng = -(-n // MAXB)
    base, rem = divmod(n, ng)
    sizes = [base + (1 if i < rem else 0) for i in range(ng)]
    a = 0
    for sz in sizes:
        fc = fc_all[a:a + sz]
        a += sz
        groups.append(dict(fc=fc, kv=[("L", fc[-1][1] + 1)]))
    return groups


def build(depth):
    NLAT = 8 + depth
    TOK = 256 + NLAT * 128
    NTF = depth * LF + 48
    O_COS = 0
    O_SIN = NLAT * 128
    O_PM = 2 * NLAT * 128
    O_MASK = O_PM + 32 * 128
    O_ID = O_MASK + 768
    O_PERM = O_ID + 128
    O_ONES = O_PERM + 128
    NTB = O_ONES + 128

    nc = bass.Bass("TRN2", target_bir_lowering=False)
    DR = {}

    def din(name, shape, dt=F32):
        DR[name] = nc.dram_tensor(name, shape, dt, kind="ExternalInput").ap()

    din("xin", [2048, TOK])
    din("w_inb", [depth * 14, 128, 2048])
    din("w_inv", [depth, 128, 16 * 256])
    din("w_inp", [depth, 128, 16 * 512])
    din("w_inz", [depth, 128, 16 * 512])
    din("w_outb", [depth * 16, 128, 2048])
    din("w_gub", [depth * 88, 128, 2048])
    din("w_dn", [depth * 44, 128, 2048])
    din("w_adab", [depth * 96, 128, 2048])
    din("poolw", [depth, 128, 512])
    din("sguw", [depth, 128, 512])
    din("tabf", [128, NTF])
    din("tabb", [128, NTB], BF16)
    out_d = nc.dram_tensor("out", [2048, 1024], F32, kind="ExternalOutput").ap()
    xs_d = nc.dram_tensor("xs", [2048, TOK], F32, kind="Internal").ap()
    xin_v = DR["xin"].rearrange("(dc p) t -> p dc t", p=128)
    xs_v = xs_d.rearrange("(dc p) t -> p dc t", p=128)
    out_v = out_d.rearrange("(dc p) t -> p dc t", p=128)

    p = Prog()
    p.stop_at = STOP_AT
    p.stop_n = STOP_N
    es = contextlib.ExitStack()

    def sb(name, shape, dt):
        return es.enter_context(nc.sbuf_tensor(name, shape, dt))

    GT = (MAXB + 1) * 128
    xg = sb("xg", [128, NDC, GT], F32)
    hm = sb("hm", [128, NDC, GT], BF16)
    qT = sb("qT", [128, 8, MAXB * 128], BF16)
    uT = sb("uT", [128, 4, MAXB * 128], BF16)
    zn = sb("zn", [128, MAXB, 512], BF16)
    hT = sb("hT", [128, 2, 4, MAXB * 128], BF16)
    kT = sb("kT", [128, 2, (NLAT + 1) * 128], BF16)
    kcT = sb("kcT", [128, 2, 256], BF16)
    v_tm = sb("v_tm", [128, NLAT + 3, 256], BF16)
    p_tm = sb("p_tm", [128, NLAT + 2, 512], BF16)
    ring = sb("ring", [128, RING * 2048], BF16)
    tabf = sb("tabf_s", [128, NTF], F32)
    tabb = sb("tabb_s", [128, NTB], BF16)
    poolw = sb("poolw_s", [128, 512], BF16)
    sguw = sb("sguw_s", [128, 512], BF16)
    sT = sb("sT", [128, 16, 2], BF16)
    modT = sb("modT", [128, depth, 96, 2], F32)
    av = sb("av", [128, depth, 2, 16, 2], F32)
    fgs = sb("fgs", [128, 16], F32)
    rstd = sb("rstd", [128, GT], F32)
    epsD = sb("epsD", [128, 2], F32)
    NSF, NSB = 6, 4
    scrF = sb("scrF", [128, NSF, 512], F32)
    scrB = sb("scrB", [128, NSB, 640], BF16)
    stat = sb("stat", [128, 8, 8], F32)
    Pb4 = sb("Pb4", [128, 4, 640], BF16)
    PTs3 = sb("PTs3", [128, 3, 640], BF16)
    otm3 = sb("otm3", [128, 3, 128], BF16)
    nsink = sb("nsink", [128, depth * 8], F32)
    bnst = sb("bnst", [128, 2, 4, 6], F32)
    bnag = sb("bnag", [128, 2, 4, 2], F32)
    pp = [es.enter_context(nc.psum_tensor(f"pp{i}", [128, 2, 512], F32)) for i in range(4)]
    ppb = [t.bitcast(BF16) for t in pp]

    def PS(b):
        return pp[b // 2][:, b % 2, :]

    def PSB(b):
        return ppb[b // 2][:, b % 2, :]

    ctr = dict(bank=0, f=0, b=0, ring=0, st=0, bn=0)

    def nbank():
        b = ctr["bank"]
        ctr["bank"] = (b + 1) % 8
        return b

    def allocF():
        i = ctr["f"]
        ctr["f"] = (i + 1) % NSF
        return ("scrF", i), scrF[:, i, :]

    def allocB():
        i = ctr["b"]
        ctr["b"] = (i + 1) % NSB
        return ("scrB", i), scrB[:, i, :]

    identb = tabb[:, O_ID:O_ID + 128]
    permb = tabb[:, O_PERM:O_PERM + 128]
    onesb = tabb[:, O_ONES:O_ONES + 128]
    TB = ("tabb",)
    TF = ("tabf",)

    def load_unit(dram2d, ncols, after=()):
        ns = ncols // 2048
        if ctr["ring"] + ns > RING:
            ctr["ring"] = 0
        s0 = ctr["ring"]
        ctr["ring"] += ns
        keys = [("ring", s0 + i) for i in range(ns)]
        for a in range(0, ns, 2):
            n = min(2, ns - a)
            dst = ring[:, (s0 + a) * 2048:(s0 + a + n) * 2048]
            src = dram2d[:, a * 2048:(a + n) * 2048]
            p.dma("pool", f"ring{s0 + a}", list(after), keys[a:a + n],
                  lambda e, dst=dst, src=src: e.dma_start(out=dst, in_=src))
        return ring[:, s0 * 2048:(s0 + ns) * 2048], keys

    def pe_mms(reads, writes, mms):
        def fn(e, mms=mms):
            ins = None
            for (o, lh, rh, st, sp) in mms:
                ins = e.matmul(o, lhsT=lh, rhs=rh, start=st, stop=sp)
            return ins
        p.op("pe", reads, writes, fn)

    def act(reads, writes, out, in_, func, bias=None, scale=None, accum_out=None):
        kw = {}
        if bias is not None:
            kw["bias"] = bias
        if scale is not None:
            kw["scale"] = scale
        if accum_out is not None:
            kw["accum_out"] = accum_out
        p.op("act", reads, writes,
             lambda e: e.activation(out=out, in_=in_, func=func, **kw))

    def dve_ts(reads, writes, out, in0, s1, s2, op0, op1=None):
        if op1 is None:
            p.op("dve", reads, writes,
                 lambda e: e.tensor_scalar(out=out, in0=in0, scalar1=s1, scalar2=None, op0=op0))
        else:
            p.op("dve", reads, writes,
                 lambda e: e.tensor_scalar(out=out, in0=in0, scalar1=s1, scalar2=s2, op0=op0, op1=op1))

    def dve_tt(reads, writes, out, in0, in1, op):
        p.op("dve", reads, writes, lambda e: e.tensor_tensor(out=out, in0=in0, in1=in1, op=op))

    def dve_stt(reads, writes, out, in0, scalar, in1, op0, op1):
        p.op("dve", reads, writes,
             lambda e: e.scalar_tensor_tensor(out=out, in0=in0, scalar=scalar, in1=in1, op0=op0, op1=op1))

    def xk(dc, s):
        return ("xg", dc, s)

    def hk(dc, s):
        return ("hm", dc, s)

    p.dma("sp", "tabf", [], [TF], lambda e: e.dma_start(out=tabf[:, :], in_=DR["tabf"]))
    p.dma("sp", "tabb", [], [TB], lambda e: e.dma_start(out=tabb[:, :], in_=DR["tabb"]))
    G0 = depth * LF
    p.op("dve", [], [("k", 0, 0), ("k", 1, 0)], lambda e: e.memset(kT[:, :, 0:128], 0.0))
    p.op("dve", [], [("v", 0)], lambda e: e.memset(v_tm[:, 0, :], 0.0))
    p.op("dve", [], [("epsD",)], lambda e: e.memset(epsD[:, 0:1], D * EPS))
    p.op("dve", [("epsD",)], [("epsD",)], lambda e: e.memset(epsD[:, 1:2], EPS))
    act([TF], [("sT",)], sT[:, :, 0], tabf[:, G0 + 16:G0 + 32], AF.Silu)
    act([TF, ("sT",)], [("sT",)], sT[:, :, 1], tabf[:, G0 + 32:G0 + 48], AF.Silu)
    dve_ts([TF], [("fgs",)], fgs[:, :], tabf[:, G0:G0 + 16], SQD, None, ALU.mult)
    for l_ in range(depth):
        dve_ts([TF], [("nsink",)], nsink[:, l_ * 8:(l_ + 1) * 8], tabf[:, l_ * LF + 136:l_ * LF + 144],
               -1.0, None, ALU.mult)

    p.stage("tables")

    def ada_chunk(l, j0, j1, bank=None):
        if bank is None:
            bank = nbank()
        nj = j1 - j0
        for j in range(j0, j1):
            u, keys = load_unit(DR["w_adab"][l * 96 + j], 2048)
            o = 2 * (j - j0)
            mms = [(PS(bank)[:, o:o + 2], u[:, dc * 128:(dc + 1) * 128], sT[:, dc, :], dc == 0, dc == 15)
                   for dc in range(16)]
            pe_mms(keys + [("sT",)], [("ps", bank)], mms)
        base = l * LF
        psv = PS(bank)[:, 0:2 * nj].rearrange("p (j r) -> p j r", r=2)
        for r in range(2):
            dve_tt([("ps", bank), TF], [("modc", l, r, j0)], modT[:, l, j0:j1, r], psv[:, :, r],
                   tabf[:, base + 32 + j0:base + 32 + j1], ALU.add)
        return [("modc", l, r, j0) for r in range(2)]

    def ada_finish(l, ckeys):
        base = l * LF
        for r in range(2):
            p.op("dve", [k for k in ckeys if k[2] == r], [("mod", l, r)], lambda e: e.engine_nop())
            for w, (goff, koff) in enumerate(((0, 16), (16, 64))):
                dve_stt([("mod", l, r), TF], [("av", l, w, r)], av[:, l, w, :, r],
                        modT[:, l, koff:koff + 16, r], 1.0, tabf[:, base + goff:base + goff + 16],
                        ALU.add, ALU.mult)
                dve_ts([("av", l, w, r)], [("av", l, w, r)], av[:, l, w, :, r], av[:, l, w, :, r],
                       SQD, None, ALU.mult)

    ck0 = []
    for j0 in range(0, 96, 32):
        ck0 += ada_chunk(0, j0, j0 + 32)
    ada_finish(0, ck0)
    p.stage("ada")

    def norm_runs(l, runs, which, dst_final=False):
        for (a, b, r, slots) in runs:
            n = b - a
            bank = nbank()
            for dc in range(16):
                kq, sq = allocB()
                act([xk(dc, s) for s in slots], [kq], sq[:, :n], xg[:, dc, a:b], AF.Square)
                pe_mms([kq, TB], [("ps", bank)], [(PS(bank)[:, :n], onesb, sq[:, :n], dc == 0, dc == 15)])
            rk = [("rstd", s) for s in slots]
            act([("ps", bank), ("epsD",)], rk, rstd[:, a:b], PS(bank)[:, :n], AF.Sqrt, bias=epsD[:, 0:1], scale=1.0)
            p.op("dve", rk, rk, lambda e, a=a, b=b: e.reciprocal(out=rstd[:, a:b], in_=rstd[:, a:b]))
            for dc in range(16):
                if dst_final:
                    dve_stt([xk(dc, s) for s in slots] + rk + [("fgs",)], [xk(dc, s) for s in slots],
                            xg[:, dc, a:b], xg[:, dc, a:b], fgs[:, dc:dc + 1], rstd[:, a:b], ALU.mult, ALU.mult)
                    continue
                kf, xn = allocF()
                dve_tt([xk(dc, s) for s in slots] + rk, [kf], xn[:, :n], xg[:, dc, a:b], rstd[:, a:b], ALU.mult)
                bcol = 0 if which == 0 else 48
                act([kf, ("av", l, which, r), ("mod", l, r)], [hk(dc, s) for s in slots],
                    hm[:, dc, a:b], xn[:, :n], AF.Identity,
                    bias=modT[:, l, bcol + dc, r:r + 1], scale=av[:, l, which, dc, r:r + 1])

    def run_layer(l):
        last = l == depth - 1
        base = l * LF
        xsrc = xin_v if l == 0 else xs_v
        xsrc_key = "xin" if l == 0 else "xs"
        p.dma("pool", "poolw", [], [("poolw",)], lambda e: e.dma_start(out=poolw[:, :], in_=DR["poolw"][l]))
        p.dma("pool", "sguw", [], [("sguw",)], lambda e: e.dma_start(out=sguw[:, :], in_=DR["sguw"][l]))
        computed = set()
        plan = layer_plan(l, depth)
        ng_fc = sum(1 for g_ in plan if g_["fc"])
        ada_state = dict(j=0, keys=[], gi=0)

        def ada_fill(target, bank=None, maxn=96):
            if l + 1 >= depth:
                return
            target = min(96, target, ada_state["j"] + maxn)
            while ada_state["j"] < target:
                j0 = ada_state["j"]
                j1 = min(target, j0 + 4)
                ada_state["keys"] += ada_chunk(l + 1, j0, j1, bank)
                ada_state["j"] = j1
        for grp in plan:
            fc, kvo = grp["fc"], grp["kv"]
            slots = fc + kvo
            ns, nfc = len(slots), len(fc)

            def col(blk):
                return blk[1] * 128 if blk[0] == "C" else 256 + blk[1] * 128

            def make_runs(sl_idx):
                runs = []
                for s in sl_idx:
                    blk = slots[s]
                    if runs and runs[-1][2] == blk[0] and runs[-1][1] == s and \
                            slots[s - 1][1] + 1 == blk[1] and runs[-1][1] - runs[-1][0] < 4:
                        runs[-1][1] = s + 1
                    else:
                        runs.append([s, s + 1, blk[0]])
                return runs

            all_runs = make_runs(range(ns))
            if nfc:
                gi = ada_state["gi"]
                ada_state["gi"] = gi + 1
                q0 = -(-96 * gi // ng_fc)
                q1 = -(-96 * (gi + 1) // ng_fc)
            for (s0, s1, kind) in all_runs:
                c0 = col(slots[s0])
                n = (s1 - s0) * 128
                for q4 in range(4):
                    keys = [xk(dc, s) for dc in range(4 * q4, 4 * q4 + 4) for s in range(s0, s1)]
                    rk = [(xsrc_key, slots[s], q4) for s in range(s0, s1)]
                    p.dma("sp", f"xg{s0}_{q4}", rk, keys,
                          lambda e, s0=s0, n=n, c0=c0, q4=q4: e.dma_start(
                              out=xg[:, 4 * q4:4 * q4 + 4, s0 * 128:s0 * 128 + n],
                              in_=xsrc[:, 4 * q4:4 * q4 + 4, c0:c0 + n]))
            nr = [(s0 * 128, s1 * 128, 1 if kind == "C" else 0, list(range(s0, s1))) for (s0, s1, kind) in all_runs]
            p.stage("xload")
            norm_runs(l, nr, 0)
            p.stage("norm1")

            need = [s for s in range(ns) if slots[s] not in computed]
            need_runs = make_runs(need)

            def hkeys(s0, s1):
                return [hk(dc, s) for dc in range(16) for s in range(s0, s1)]

            for kh in range(2):
                aft = [xk(dc, s) for dc in (3, 7, 11, 15) for s in range(ns)] if kh == 0 else ()
                u, ukeys = load_unit(DR["w_inb"][l * 14 + 8 + kh], 2048, after=aft)
                for (s0, s1, kind) in need_runs:
                    n = (s1 - s0) * 128
                    a = s0 * 128
                    bank = nbank()
                    if kh == 0:
                        for dc in range(16):
                            pe_mms(ukeys + [hk(dc, s) for s in range(s0, s1)], [("ps", bank)],
                                   [(PS(bank)[:, :n], u[:, dc * 128:(dc + 1) * 128], hm[:, dc, a:a + n],
                                     dc == 0, dc == 15)])
                    else:
                        pe_mms(ukeys + hkeys(s0, s1), [("ps", bank)],
                               [(PS(bank)[:, :n], u[:, dc * 128:(dc + 1) * 128], hm[:, dc, a:a + n], dc == 0, dc == 15)
                                for dc in range(16)])
                    if kind == "C":
                        c0 = slots[s0][1] * 128
                        act([("ps", bank)], [("kc", kh)], kcT[:, kh, c0:c0 + n], PS(bank)[:, :n], AF.Copy)
                    else:
                        li = slots[s0][1]
                        dst = kT[:, kh, (li + 1) * 128:(li + 1) * 128 + n]
                        dkeys = [("k", kh, li + 1 + i) for i in range(s1 - s0)]
                        rope_evac(bank, n, li, dst, dkeys)
            p.stage("K")
            u, ukeys = load_unit(DR["w_inv"][l], 4096)
            for s in need:
                blk = slots[s]
                vs = NLAT + 1 + blk[1] if blk[0] == "C" else blk[1] + 1
                bank = nbank()
                pe_mms(ukeys + hkeys(s, s + 1), [("ps", bank)],
                       [(PS(bank)[:, :256], hm[:, dc, s * 128:(s + 1) * 128], u[:, dc * 256:(dc + 1) * 256],
                         dc == 0, dc == 15) for dc in range(16)])
                act([("ps", bank)], [("v", vs)], v_tm[:, vs, :], PS(bank)[:, :256], AF.Copy)
            p.stage("V")
            if not (last and nfc == 0):
                u, ukeys = load_unit(DR["w_inp"][l], 8192)
                for s in need:
                    blk = slots[s]
                    ps_ = blk[1] if blk[0] == "C" else 2 + blk[1]
                    bank = nbank()
                    pe_mms(ukeys + hkeys(s, s + 1), [("ps", bank)],
                           [(PS(bank)[:, :], hm[:, dc, s * 128:(s + 1) * 128], u[:, dc * 512:(dc + 1) * 512],
                             dc == 0, dc == 15) for dc in range(16)])
                    p.op("dve", [("ps", bank)], [("p", ps_)],
                         lambda e, ps_=ps_, bank=bank: e.tensor_copy(out=p_tm[:, ps_, :], in_=PS(bank)[:, :]))
            p.stage("P")
            for s in need:
                computed.add(slots[s])
            if nfc == 0:
                continue
            nf = nfc * 128
            fc_runs = make_runs(range(nfc))
            for h in range(8):
                u, ukeys = load_unit(DR["w_inb"][l * 14 + h], 2048)
                bank = nbank()
                pe_mms(ukeys + hkeys(0, nfc), [("ps", bank)],
                       [(PS(bank)[:, :nf], u[:, dc * 128:(dc + 1) * 128], hm[:, dc, 0:nf], dc == 0, dc == 15)
                        for dc in range(16)])
                for (s0, s1, kind) in fc_runs:
                    n = (s1 - s0) * 128
                    a = s0 * 128
                    dkeys = [("q", h, s) for s in range(s0, s1)]
                    if kind == "C":
                        act([("ps", bank)], dkeys, qT[:, h, a:a + n], PS(bank)[:, a:a + n], AF.Copy)
                    else:
                        rope_evac(bank, n, slots[s0][1], qT[:, h, a:a + n], dkeys, poff=a)
            p.stage("Q")
            for g in range(4):
                u, ukeys = load_unit(DR["w_inb"][l * 14 + 10 + g], 2048)
                bank = nbank()
                pe_mms(ukeys + hkeys(0, nfc), [("ps", bank)],
                       [(PS(bank)[:, :nf], u[:, dc * 128:(dc + 1) * 128], hm[:, dc, 0:nf], dc == 0, dc == 15)
                        for dc in range(16)])
                act([("ps", bank)], [("u", g, s) for s in range(nfc)], uT[:, g, 0:nf], PS(bank)[:, :nf], AF.Gelu)
            p.stage("U")
            u, ukeys = load_unit(DR["w_inz"][l], 8192)
            for s in range(nfc):
                bank = nbank()
                pe_mms(ukeys + hkeys(s, s + 1), [("ps", bank)],
                       [(PS(bank)[:, :], hm[:, dc, s * 128:(s + 1) * 128], u[:, dc * 512:(dc + 1) * 512],
                         dc == 0, dc == 15) for dc in range(16)])
                kf, gz = allocF()
                act([("ps", bank)], [kf], gz[:, :], PS(bank)[:, :], AF.Gelu)
                bi = ctr["bn"]
                ctr["bn"] = (bi + 1) % 2
                for g in range(4):
                    p.op("dve", [kf], [("bnst", bi, g)],
                         lambda e, g=g, bi=bi, gz=gz: e.bn_stats(out=bnst[:, bi, g, :], in_=gz[:, g * 128:(g + 1) * 128]))
                    p.op("dve", [("bnst", bi, g)], [("bnag", bi, g)],
                         lambda e, g=g, bi=bi: e.bn_aggr(out=bnag[:, bi, g, :], in_=bnst[:, bi, g, :]))
                bk = [("bnag", bi, g) for g in range(4)]
                act(bk + [("epsD",)], bk, bnag[:, bi, :, 1], bnag[:, bi, :, 1], AF.Sqrt, bias=epsD[:, 1:2], scale=1.0)
                p.op("dve", bk, bk, lambda e, bi=bi: e.reciprocal(out=bnag[:, bi, :, 1], in_=bnag[:, bi, :, 1]))
                for g in range(4):
                    dve_ts([kf, ("bnag", bi, g)], [("zn", s, g)], zn[:, s, g * 128:(g + 1) * 128],
                           gz[:, g * 128:(g + 1) * 128], bnag[:, bi, g, 0:1], bnag[:, bi, g, 1:2],
                           ALU.subtract, ALU.mult)

            p.stage("Z")
            pool_st = []
            for s in range(nfc):
                blk = slots[s]
                if blk[0] == "C":
                    me = blk[1]
                    srcs = [(0, 4)] if me == 1 else []
                    srcs += [(me, 6 + me)]
                    srcs += [(1, 5)] if me == 0 else []
                else:
                    li = blk[1]
                    srcs = [(2 + li - 1, 0)] if li > 0 else []
                    srcs += [(2 + li, 3 if li == 0 else 1)]
                    srcs += [(2 + li + 1, 2)]
                bank = nbank()
                mms = []
                for g in range(4):
                    for j, (pslot, kind) in enumerate(srcs):
                        mo = O_PM + (g * 8 + kind) * 128
                        mms.append((PS(bank)[:, g * 128:(g + 1) * 128], p_tm[:, pslot, g * 128:(g + 1) * 128],
                                    tabb[:, mo:mo + 128], j == 0, j == len(srcs) - 1))
                pe_mms([("p", ps_) for ps_, _ in srcs] + [TB], [("ps", bank)], mms)
                kb, pl = ("Pb", s), Pb4[:, s, :]
                act([("ps", bank)], [kb], pl[:, 0:512], PS(bank)[:, :], AF.Copy)
                pool_st.append((kb, pl))
            sgu_st = []
            for s in range(nfc):
                bank = nbank()
                pe_mms([("zn", s, g) for g in range(4)] + [("sguw",)], [("ps", bank)],
                       [(PS(bank)[:, g * 128:(g + 1) * 128], zn[:, s, g * 128:(g + 1) * 128],
                         sguw[:, g * 128:(g + 1) * 128], True, True) for g in range(4)])
                sgu_st.append(bank)
            for s in range(nfc):
                bank = sgu_st[s]
                kf, tmp = allocF()
                for g in range(4):
                    dve_stt([("ps", bank), TF], [kf], tmp[:, g * 128:(g + 1) * 128],
                            PS(bank)[:, g * 128:(g + 1) * 128], tabf[:, base + 132 + g:base + 133 + g],
                            tabf[:, base + 144 + g * 128:base + 144 + (g + 1) * 128], ALU.mult, ALU.add)
                for g in range(4):
                    dve_tt([kf, ("u", g, s)], [hk(12 + g, s)], hm[:, 12 + g, s * 128:(s + 1) * 128],
                           tmp[:, g * 128:(g + 1) * 128], uT[:, g, s * 128:(s + 1) * 128], ALU.mult)
            for s in range(nfc):
                kb, pl = pool_st[s]
                bank2 = nbank()
                pe_mms([kb, ("poolw",)], [("ps", bank2)],
                       [(PS(bank2)[:, g * 128:(g + 1) * 128], poolw[:, g * 128:(g + 1) * 128],
                         pl[:, g * 128:(g + 1) * 128], True, True) for g in range(4)])
                for g in range(4):
                    act([("ps", bank2), TF], [hk(8 + g, s)], hm[:, 8 + g, s * 128:(s + 1) * 128],
                        PS(bank2)[:, g * 128:(g + 1) * 128], AF.Identity,
                        scale=tabf[:, base + 128 + g:base + 129 + g])

            items = [(s, h) for s in range(nfc) for h in range(8)]
            NI = len(items)
            ist = [dict() for _ in items]

            def st_A(i):
                s, h = items[i]
                kh = h // 4
                blk = slots[s]
                pi = i % 2
                qa = qT[:, h, s * 128:(s + 1) * 128]
                wr = [("ps", 2 * pi), ("ps", 2 * pi + 1)]
                if blk[0] == "C":
                    pe_mms([("q", h, s), ("kc", kh)], wr,
                           [(pp[pi][:, 0, 0:256], qa, kcT[:, kh, :], True, True)])
                else:
                    li = blk[1]
                    c0 = li * 128
                    mo = O_MASK + (384 if li == 0 else 0)
                    mk = tabb[:, mo:mo + 384]
                    pe_mms([("q", h, s), ("kc", kh), TB] + [("k", kh, li + i2) for i2 in range(3)], wr,
                           [(pp[pi][:, 0, 0:320], qa, kT[:, kh, c0:c0 + 320], True, False),
                            (pp[pi][:, 0, 0:320], identb, mk[:, 0:320], False, True),
                            (pp[pi][:, 1, 0:64], qa, kT[:, kh, c0 + 320:c0 + 384], True, False),
                            (pp[pi][:, 1, 0:64], identb, mk[:, 320:384], False, True),
                            (pp[pi][:, 1, 64:320], qa, kcT[:, kh, :], True, True)])

            def st_B(i):
                s, h = items[i]
                blk = slots[s]
                pi = i % 2
                isC = blk[0] == "C"
                rd = [("ps", 2 * pi), ("ps", 2 * pi + 1)]
                si = ctr["st"]
                ctr["st"] = (si + 1) % 8
                sk = ("stat", si)
                d = ist[i]
                d["si"] = si
                if isC:
                    sv = pp[pi][:, 0, 0:256]
                    nk = 256
                    p.op("dve", rd, [sk], lambda e: e.reduce_max(out=stat[:, si, 0:1], in_=sv, axis=AX.X))
                else:
                    sv = pp[pi][:, :, 0:320]
                    nk = 640
                    p.op("dve", rd, [sk], lambda e: e.reduce_max(out=stat[:, si, 0:1], in_=sv, axis=AX.XY))
                d["nk"] = nk
                dve_ts([sk], [sk], stat[:, si, 2:3], stat[:, si, 0:1], -SCALE, None, ALU.mult)
                pbi = i % 4
                kb, Pb = ("Pb", pbi), Pb4[:, pbi, :]
                d["kb"], d["Pb"] = kb, Pb
                if isC:
                    pout = Pb[:, 0:256]
                else:
                    pout = Pb[:, 0:640].rearrange("p (a b) -> p a b", a=2)
                sink_ap = tabf[:, base + 136 + h:base + 137 + h]
                act(rd + [sk], [kb, ("stat_rs", si)], pout, sv, AF.Exp, bias=stat[:, si, 2:3], scale=SCALE,
                    accum_out=stat[:, si, 3:4])
                act([sk, TF], [("stat_es", si)], stat[:, si, 4:5], stat[:, si, 2:3], AF.Exp, bias=sink_ap, scale=1.0)

            def st_C(i):
                d = ist[i]
                tb = 4
                nb = d["nk"] // 128
                Pb = d["Pb"]
                pe_trs([d["kb"], TB], [("ps", tb)],
                       [(PSB(tb)[:, j * 128:(j + 1) * 128], Pb[:, j * 128:(j + 1) * 128]) for j in range(nb)])
                pti = i % 3
                kb2, PTs = ("PTs", pti), PTs3[:, pti, :]
                d["kb2"], d["PTs"] = kb2, PTs
                act([("ps", tb)], [kb2], PTs[:, :d["nk"]], PSB(tb)[:, :d["nk"]], AF.Copy)

            def st_E(i):
                s, h = items[i]
                kh = h // 4
                blk = slots[s]
                d = ist[i]
                si = d["si"]
                nb = d["nk"] // 128
                PTs = d["PTs"]
                if blk[0] == "C":
                    vsl = [NLAT + 1, NLAT + 2]
                else:
                    li = blk[1]
                    vsl = [li, li + 1, li + 2, NLAT + 1, NLAT + 2]
                ob = 6 + i % 2
                pe_mms([d["kb2"]] + [("v", v) for v in vsl], [("ps", ob)],
                       [(PS(ob)[:, 0:128], PTs[:, j * 128:(j + 1) * 128],
                         v_tm[:, vsl[j], kh * 128:(kh + 1) * 128], j == 0, j == nb - 1) for j in range(nb)])
                dve_tt([("stat_rs", si), ("stat_es", si)], [("stat_d", si)], stat[:, si, 5:6], stat[:, si, 3:4],
                       stat[:, si, 4:5], ALU.add)
                p.op("dve", [("stat_d", si)], [("stat_r", si)],
                     lambda e: e.reciprocal(out=stat[:, si, 6:7], in_=stat[:, si, 5:6]))
                oi = i % 3
                kb3, otm = ("otm", oi), otm3[:, oi, :]
                d["kb3"], d["otm"] = kb3, otm
                dve_ts([("ps", ob), ("stat_r", si)], [kb3], otm[:, 0:128], PS(ob)[:, 0:128],
                       stat[:, si, 6:7], None, ALU.mult)

            def st_G(i):
                s, h = items[i]
                d = ist[i]
                ob = 6 + i % 2
                pe_trs([d["kb3"], TB], [("ps", ob)], [(PSB(ob)[:, 512:640], d["otm"][:, 0:128])])
                p.op("dve", [("ps", ob)], [hk(h, s)],
                     lambda e, h=h, s=s, ob=ob: e.tensor_copy(out=hm[:, h, s * 128:(s + 1) * 128],
                                                              in_=PSB(ob)[:, 512:640]))

            for t in range(NI + 4):
                if 0 <= t - 4 < NI:
                    st_G(t - 4)
                if 0 <= t - 3 < NI:
                    st_E(t - 3)
                if 0 <= t - 2 < NI:
                    st_C(t - 2)
                if t % 4 == 1:
                    ada_fill(q1, bank=5, maxn=3)
                if t < NI:
                    st_A(t)
                    st_B(t)

            p.stage("attn")
            p.stage("sgu")
            def resid(bank, j, gate_k):
                for (s0, s1, kind) in fc_runs:
                    r = 1 if kind == "C" else 0
                    a, b = s0 * 128, s1 * 128
                    keys = [xk(j, s) for s in range(s0, s1)]
                    dve_stt([("ps", bank), ("mod", l, r)] + keys, keys, xg[:, j, a:b], PS(bank)[:, a:b],
                            modT[:, l, gate_k + j, r:r + 1], xg[:, j, a:b], ALU.mult, ALU.add)

            for j in range(16):
                u, ukeys = load_unit(DR["w_outb"][l * 16 + j], 2048)
                bank = nbank()
                pe_mms(ukeys + hkeys(0, nfc), [("ps", bank)],
                       [(PS(bank)[:, :nf], u[:, mc * 128:(mc + 1) * 128], hm[:, mc, 0:nf], mc == 0, mc == 15)
                        for mc in range(16)])
                resid(bank, j, 32)

            p.stage("outproj")
            nr2 = [(s0 * 128, s1 * 128, 1 if kind == "C" else 0, list(range(s0, s1))) for (s0, s1, kind) in fc_runs]
            norm_runs(l, nr2, 1)
            ada_fill(-(-96 * ada_state["gi"] // ng_fc))

            p.stage("norm2")
            for G in range(11):
                hb = G % 2
                for c in range(4):
                    ci = G * 4 + c
                    ug, gkeys = load_unit(DR["w_gub"][l * 88 + 2 * ci], 2048)
                    uu, ukeys2 = load_unit(DR["w_gub"][l * 88 + 2 * ci + 1], 2048)
                    bg, bu = nbank(), nbank()
                    if ci == 0:
                        for dc in range(16):
                            pe_mms(gkeys + [hk(dc, s) for s in range(nfc)], [("ps", bg)],
                                   [(PS(bg)[:, :nf], ug[:, dc * 128:(dc + 1) * 128], hm[:, dc, 0:nf],
                                     dc == 0, dc == 15)])
                    else:
                        pe_mms(gkeys + hkeys(0, nfc), [("ps", bg)],
                               [(PS(bg)[:, :nf], ug[:, dc * 128:(dc + 1) * 128], hm[:, dc, 0:nf], dc == 0, dc == 15)
                                for dc in range(16)])
                    pe_mms(ukeys2 + hkeys(0, nfc), [("ps", bu)],
                           [(PS(bu)[:, :nf], uu[:, dc * 128:(dc + 1) * 128], hm[:, dc, 0:nf], dc == 0, dc == 15)
                            for dc in range(16)])
                    kf, sg = allocF()
                    act([("ps", bg)], [kf], sg[:, :nf], PS(bg)[:, :nf], AF.Silu)
                    dve_tt([kf, ("ps", bu)], [("hT", hb, c)], hT[:, hb, c, 0:nf], sg[:, :nf], PS(bu)[:, :nf], ALU.mult)
                wd = []
                for c in range(4):
                    wd.append(load_unit(DR["w_dn"][l * 44 + G * 4 + c], 2048))
                for j in range(16):
                    bank = nbank()
                    rd = [("hT", hb, c) for c in range(4)]
                    for c in range(4):
                        rd += wd[c][1]
                    pe_mms(rd, [("ps", bank)],
                           [(PS(bank)[:, :nf], wd[c][0][:, j * 128:(j + 1) * 128], hT[:, hb, c, 0:nf], c == 0, c == 3)
                            for c in range(4)])
                    resid(bank, j, 80)

            p.stage("ffn")
            if not last:
                for (s0, s1, kind) in fc_runs:
                    c0 = col(slots[s0])
                    n = (s1 - s0) * 128
                    for q4 in range(4):
                        keys = [xk(dc, s) for dc in range(4 * q4, 4 * q4 + 4) for s in range(s0, s1)]
                        wk = [("xs", slots[s], q4) for s in range(s0, s1)]
                        p.dma("sp", f"xst{s0}_{q4}", keys, wk,
                              lambda e, s0=s0, n=n, c0=c0, q4=q4: e.dma_start(
                                  out=xs_v[:, 4 * q4:4 * q4 + 4, c0:c0 + n],
                                  in_=xg[:, 4 * q4:4 * q4 + 4, s0 * 128:s0 * 128 + n]))
            else:
                norm_runs(l, nr2, 0, dst_final=True)
                for (s0, s1, kind) in fc_runs:
                    c0 = slots[s0][1] * 128
                    n = (s1 - s0) * 128
                    keys = [xk(dc, s) for dc in range(16) for s in range(s0, s1)]
                    p.dma("sp", f"xst{s0}", keys, [("out", slots[s]) for s in range(s0, s1)],
                          lambda e, s0=s0, n=n, c0=c0: e.dma_start(out=out_v[:, :, c0:c0 + n],
                                                                   in_=xg[:, :, s0 * 128:s0 * 128 + n]))
        if l + 1 < depth:
            assert ada_state["j"] == 96
            ada_finish(l + 1, ada_state["keys"])

    def pe_trs(reads, writes, trs):
        def fn(e, trs=trs):
            ins = None
            for (o, i_) in trs:
                ins = e.transpose(out=o, in_=i_, identity=identb)
            return ins
        p.op("pe", reads, writes, fn)

    def rope_evac(bank, n, li, dst, dkeys, poff=0):
        src = PS(bank)[:, poff:poff + n]
        kb, kraw = allocB()
        act([("ps", bank)], [kb], kraw[:, :n], src, AF.Copy)
        kf1, t1 = allocF()
        dve_tt([("ps", bank), TB], [kf1], t1[:, :n], src, tabb[:, O_COS + li * 128:O_COS + li * 128 + n], ALU.mult)
        b2 = nbank()
        pe_mms([kb, TB], [("ps", b2)], [(PS(b2)[:, :n], permb, kraw[:, :n], True, True)])
        kf2, t2 = allocF()
        dve_tt([("ps", b2), TB], [kf2], t2[:, :n], PS(b2)[:, :n], tabb[:, O_SIN + li * 128:O_SIN + li * 128 + n], ALU.mult)
        dve_tt([kf1, kf2], dkeys, dst, t1[:, :n], t2[:, :n], ALU.add)

    for l in range(depth):
        run_layer(l)

    semnames = ["pe", "act", "dve"] + sorted(p.dcnt.keys())
    SEM = {n_: es.enter_context(nc.semaphore(f"s_{n_}")) for n_ in semnames}
    finals = [(k, v) for k, v in p.dcnt.items() if k.startswith("xst")]

    def replay(eng, e):
        for waits, fn, (sk, inc) in p.ops[eng]:
            for k, v in waits:
                e.wait_ge(SEM[k], v)
            fn(e).then_inc(SEM[sk], inc)
        if eng == "sp":
            for k, v in finals:
                e.wait_ge(SEM[k], v)

    with es:
        with nc.Block() as block:
            @block.tensor
            def _(e):
                replay("pe", e)

            @block.scalar
            def _(e):
                replay("act", e)

            @block.vector
            def _(e):
                replay("dve", e)

            @block.gpsimd
            def _(e):
                replay("pool", e)

            @block.sync
            def _(e):
                replay("sp", e)
    return nc


def _unitize(W):
    nj = W.shape[1] // 128
    return np.ascontiguousarray(W.reshape(16, 128, nj, 128).transpose(2, 1, 0, 3)).reshape(nj, 128, 2048)


def _rowblock(W, c0, c1):
    C = c1 - c0
    return np.ascontiguousarray(W[:, c0:c1].reshape(16, 128, C).transpose(1, 0, 2)).reshape(128, 16 * C)


def _pp(v, n):
    return np.ascontiguousarray(np.asarray(v, np.float32).reshape(n, 128).T)


def _pool_mats(mirror, S=2048):
    out = np.zeros((4, 8, 128, 128), np.float32)
    for g, w in enumerate((2, 4, 8, 16)):
        half = w // 2
        idx = np.arange(384)
        gl = (S - 1 - idx) if mirror else idx

        def dense(gpos, Sq):
            lo = np.clip(gpos - half, 0, Sq)
            hi = np.clip(gpos + half, 0, Sq)
            cnt = (hi - lo).astype(np.float32)
            inw = (gpos[:, None] >= lo[None, :]) & (gpos[:, None] < hi[None, :])
            M = np.where(inw, (np.float32(1.0) / cnt)[None, :], np.float32(0.0)).astype(np.float32)
            M = M - np.eye(len(gpos), dtype=np.float32)
            return M

        M = dense(gl, S)
        out[g, 0] = M[0:128, 128:256]
        out[g, 1] = M[128:256, 128:256]
        out[g, 2] = M[256:384, 128:256]
        out[g, 3] = M[0:128, 0:128]
        Mc = dense((255 - np.arange(256)) if mirror else np.arange(256), 256)
        out[g, 4] = Mc[0:128, 128:256]
        out[g, 5] = Mc[128:256, 0:128]
        out[g, 6] = Mc[0:128, 0:128]
        out[g, 7] = Mc[128:256, 128:256]
    return out


def _host_prep(inputs, depth):
    f32 = np.float32
    NLAT = 8 + depth
    g = {k: np.asarray(v) for k, v in inputs.items()}
    shared = {}
    w_in = g["w_in"][:depth]
    shared["w_inb"] = np.concatenate(
        [np.concatenate([_unitize(w_in[l][:, 0:1280]), _unitize(w_in[l][:, 2048:2560])], 0) for l in range(depth)], 0)
    shared["w_inv"] = np.stack([_rowblock(w_in[l], 1280, 1536) for l in range(depth)])
    shared["w_inp"] = np.stack([_rowblock(w_in[l], 1536, 2048) for l in range(depth)])
    shared["w_inz"] = np.stack([_rowblock(w_in[l], 2560, 3072) for l in range(depth)])
    shared["w_outb"] = np.concatenate([_unitize(g["w_out"][l]) for l in range(depth)], 0)
    gub = []
    for l in range(depth):
        wg = _unitize(g["w_gate_up"][l][:, :5632])
        wu = _unitize(g["w_gate_up"][l][:, 5632:])
        gub.append(np.stack([wg, wu], 1).reshape(88, 128, 2048))
    shared["w_gub"] = np.concatenate(gub, 0)
    shared["w_dn"] = np.ascontiguousarray(g["w_down"][:depth].reshape(depth * 44, 128, 2048))
    shared["w_adab"] = np.concatenate([_unitize(g["w_ada"][l]) for l in range(depth)], 0)
    shared["poolw"] = np.ascontiguousarray(g["pool_w"][:depth].transpose(0, 2, 1, 3)).reshape(depth, 128, 512)
    sguw_n = np.ascontiguousarray(g["sgu_w"][:depth].transpose(0, 3, 1, 2)).reshape(depth, 128, 512)
    sguw_m = np.ascontiguousarray(g["sgu_w"][:depth][:, :, ::-1, ::-1].transpose(0, 3, 1, 2)).reshape(depth, 128, 512)

    tab_common = np.zeros((128, depth * LF + 48), f32)
    for l in range(depth):
        b0 = l * LF
        tab_common[:, b0:b0 + 16] = _pp(g["norm_mix_g"][l], 16)
        tab_common[:, b0 + 16:b0 + 32] = _pp(g["norm_ffn_g"][l], 16)
        tab_common[:, b0 + 32:b0 + 128] = _pp(g["b_ada"][l], 96)
        tab_common[:, b0 + 128:b0 + 132] = _pp(g["pool_scale"][l], 4)
        tab_common[:, b0 + 132:b0 + 136] = np.ascontiguousarray(g["sgu_norm_g"][l].T)
        tab_common[:, b0 + 136:b0 + 144] = np.broadcast_to(g["attn_sink"][l][None, :], (128, 8))
        tab_common[:, b0 + 144:b0 + 656] = np.broadcast_to(g["sgu_b"][l].reshape(1, 512), (128, 512))
    G0 = depth * LF
    tab_common[:, G0:G0 + 16] = _pp(g["final_norm_g"], 16)
    tab_common[:, G0 + 32:G0 + 48] = _pp(g["c_ctx"], 16)

    inv_freq = (10000.0 ** (-np.arange(0, 64, 2, dtype=f32) / f32(64))).astype(f32)
    d = np.arange(128)
    axis, half, fr = d // 64, (d % 64) // 32, d % 32
    qi = np.arange(128)[:, None]
    kj = np.arange(128)[None, :]
    m_prev = np.where(kj >= qi, 0.0, NEG).astype(f32)
    m_next = np.where(kj <= qi, 0.0, NEG).astype(f32)
    zero = np.zeros((128, 128), f32)
    mask = np.concatenate([m_prev, zero, m_next, np.full((128, 128), NEG, f32), zero, m_next], 1)
    ident = np.eye(128, dtype=f32)
    partner = np.where((d % 64) < 32, d + 32, d - 32)
    perm = np.zeros((128, 128), f32)
    perm[partner, d] = 1.0
    ones = np.ones((128, 128), f32)

    in_maps = []
    for b in range(4):
        for h in range(2):
            loc = np.arange(NLAT * 128)
            gidx = loc if h == 0 else 2047 - loc
            xt = np.empty((2048, 256 + NLAT * 128), f32)
            xt[:, :256] = (g["ctx"][b] if h == 0 else g["ctx"][b][::-1]).T
            xt[:, 256:] = g["x"][b][gidx].T
            tabf = tab_common.copy()
            tabf[:, G0 + 16:G0 + 32] = _pp(g["c"][b], 16)
            if h == 1:
                for l in range(depth):
                    b0 = l * LF
                    tabf[:, b0 + 144:b0 + 656] = np.broadcast_to(g["sgu_b"][l][:, ::-1].reshape(1, 512), (128, 512))
            row = (gidx // 64).astype(f32)
            colp = (gidx % 64).astype(f32)
            pos = np.where(axis[:, None] == 0, row[None, :], colp[None, :]).astype(f32)
            ang = (pos * inv_freq[fr][:, None]).astype(f32)
            cos_t = np.cos(ang).astype(f32)
            sin_t = np.sin(ang).astype(f32) * np.where(half == 0, -1.0, 1.0).astype(f32)[:, None]
            pm = _pool_mats(h == 1).reshape(32 * 128, 128).reshape(32, 128, 128).transpose(1, 0, 2).reshape(128, 32 * 128)
            tabb = np.concatenate([cos_t, sin_t, pm, mask, ident, perm, ones], 1).astype(ml_dtypes.bfloat16)
            m = dict(shared)
            m["xin"] = xt
            m["sguw"] = sguw_n if h == 0 else sguw_m
            m["tabf"] = tabf
            m["tabb"] = tabb
            in_maps.append(m)
    return in_maps


_NC_CACHE = {}


def kernel(**inputs):
    depth = DEPTH
    in_maps = _host_prep(inputs, depth)
    if depth not in _NC_CACHE:
        _NC_CACHE[depth] = build(depth)
    nc = _NC_CACHE[depth]
    res = run_bass_kernel_spmd(nc, in_maps, core_ids=list(range(8)))
    y = np.empty((4, 2048, 2048), np.float32)
    for b in range(4):
        for h in range(2):
            o = np.asarray(res.results[b * 2 + h]["out"]).T
            if h == 0:
                y[b, 0:1024] = o
            else:
                y[b, 1024:2048] = o[::-1]
    return y
```

```python
import contextlib
import math

import ml_dtypes
import numpy as np

import concourse.bass as bass
import concourse.mybir as mybir
from concourse.bass_utils import run_bass_kernel_spmd

F32 = mybir.dt.float32
BF16 = mybir.dt.bfloat16
AF = mybir.ActivationFunctionType
ALU = mybir.AluOpType
AX = mybir.AxisListType

DEPTH = 4
D = 2048
NDC = 16
EPS = 1e-6
SCALE = 128 ** -0.5
SQD = math.sqrt(2048.0)
NEG = -30000.0
RING = 8
LF = 16 + 16 + 96 + 4 + 4 + 8 + 512
MAXB = 4
STOP_AT = None
STOP_N = 1


class Prog:
    ENG = ("pe", "act", "dve", "pool", "sp")

    def __init__(self):
        self.ops = {e: [] for e in self.ENG}
        self.cnt = {e: 0 for e in self.ENG}
        self.dcnt = {}
        self.seen = {e: {} for e in self.ENG}
        self.lastw = {}
        self.readers = {}
        self.enabled = True
        self.stop_at = None
        self.stop_n = 1

    def stage(self, name):
        if self.stop_at is not None and name == self.stop_at:
            self.stop_n -= 1
            if self.stop_n <= 0:
                self.enabled = False

    def _deps(self, eng, reads, writes):
        deps = {}

        def add(k, v):
            if deps.get(k, 0) < v:
                deps[k] = v

        for r in reads:
            w = self.lastw.get(r)
            if w:
                add(*w)
            if r[0] == "ps":
                for k, v in self.readers.get(r, {}).items():
                    if k != eng:
                        add(k, v)
        for r in writes:
            w = self.lastw.get(r)
            if w:
                add(*w)
            for k, v in self.readers.get(r, {}).items():
                add(k, v)
        out = []
        for k, v in deps.items():
            if k == "pe" and eng == "pe":
                continue
            if self.seen[eng].get(k, 0) >= v:
                continue
            self.seen[eng][k] = v
            out.append((k, v))
        return out

    def _commit(self, reads, writes, tag):
        k, v = tag
        for r in writes:
            self.lastw[r] = tag
            self.readers[r] = {}
        for r in reads:
            d = self.readers.setdefault(r, {})
            if d.get(k, 0) < v:
                d[k] = v

    def op(self, eng, reads, writes, fn):
        if not self.enabled:
            return
        waits = self._deps(eng, reads, writes)
        self.cnt[eng] += 1
        self.ops[eng].append((waits, fn, (eng, 1)))
        self._commit(reads, writes, (eng, self.cnt[eng]))

    def dma(self, qeng, semname, reads, writes, fn):
        if not self.enabled:
            return
        waits = self._deps(qeng, reads, writes)
        self.dcnt[semname] = self.dcnt.get(semname, 0) + 16
        self.ops[qeng].append((waits, fn, (semname, 16)))
        self._commit(reads, writes, (semname, self.dcnt[semname]))


def layer_plan(l, depth):
    last = l == depth - 1
    NL = 8 + (depth - 1 - l)
    lat = [("L", i) for i in range(NL)]
    ctx = [("C", 0), ("C", 1)]
    fc_all = lat if last else ctx + lat
    groups = []
    if last:
        groups.append(dict(fc=[], kv=list(ctx)))
    n = len(fc_all)
    ng = -(-n // MAXB)
    base, rem = divmod(n, ng)
    sizes = [base + (1 if i < rem else 0) for i in range(ng)]
    a = 0
    for sz in sizes:
        fc = fc_all[a:a + sz]
        a += sz
        groups.append(dict(fc=fc, kv=[("L", fc[-1][1] + 1)]))
    return groups


def build(depth):
    NLAT = 8 + depth
    TOK = 256 + NLAT * 128
    NTF = depth * LF + 48
    O_COS = 0
    O_SIN = NLAT * 128
    O_PM = 2 * NLAT * 128
    O_MASK = O_PM + 32 * 128
    O_ID = O_MASK + 768
    O_PERM = O_ID + 128
    O_ONES = O_PERM + 128
    NTB = O_ONES + 128

    nc = bass.Bass("TRN2", target_bir_lowering=False)
    DR = {}

    def din(name, shape, dt=F32):
        DR[name] = nc.dram_tensor(name, shape, dt, kind="ExternalInput").ap()

    din("xin", [2048, TOK])
    din("w_inb", [depth * 14, 128, 2048])
    din("w_inv", [depth, 128, 16 * 256])
    din("w_inp", [depth, 128, 16 * 512])
    din("w_inz", [depth, 128, 16 * 512])
    din("w_outb", [depth * 16, 128, 2048])
    din("w_gub", [depth * 88, 128, 2048])
    din("w_dn", [depth * 44, 128, 2048])
    din("w_adab", [depth * 96, 128, 2048])
    din("poolw", [depth, 128, 512])
    din("sguw", [depth, 128, 512])
    din("tabf", [128, NTF])
    din("tabb", [128, NTB], BF16)
    out_d = nc.dram_tensor("out", [2048, 1024], F32, kind="ExternalOutput").ap()
    xs_d = nc.dram_tensor("xs", [2048, TOK], F32, kind="Internal").ap()
    xin_v = DR["xin"].rearrange("(dc p) t -> p dc t", p=128)
    xs_v = xs_d.rearrange("(dc p) t -> p dc t", p=128)
    out_v = out_d.rearrange("(dc p) t -> p dc t", p=128)

    p = Prog()
    p.stop_at = STOP_AT
    p.stop_n = STOP_N
    es = contextlib.ExitStack()

    def sb(name, shape, dt):
        return es.enter_context(nc.sbuf_tensor(name, shape, dt))

    GT = (MAXB + 1) * 128
    xg = sb("xg", [128, NDC, GT], F32)
    hm = sb("hm", [128, NDC, GT], BF16)
    qT = sb("qT", [128, 8, MAXB * 128], BF16)
    uT = sb("uT", [128, 4, MAXB * 128], BF16)
    zn = sb("zn", [128, MAXB, 512], BF16)
    hT = sb("hT", [128, 2, 4, MAXB * 128], BF16)
    kT = sb("kT", [128, 2, (NLAT + 1) * 128], BF16)
    kcT = sb("kcT", [128, 2, 256], BF16)
    v_tm = sb("v_tm", [128, NLAT + 3, 256], BF16)
    p_tm = sb("p_tm", [128, NLAT + 2, 512], BF16)
    ring = sb("ring", [128, RING * 2048], BF16)
    tabf = sb("tabf_s", [128, NTF], F32)
    tabb = sb("tabb_s", [128, NTB], BF16)
    poolw = sb("poolw_s", [128, 512], BF16)
    sguw = sb("sguw_s", [128, 512], BF16)
    sT = sb("sT", [128, 16, 2], BF16)
    modT = sb("modT", [128, depth, 96, 2], F32)
    av = sb("av", [128, depth, 2, 16, 2], F32)
    fgs = sb("fgs", [128, 16], F32)
    rstd = sb("rstd", [128, GT], F32)
    epsD = sb("epsD", [128, 2], F32)
    NSF, NSB = 6, 4
    scrF = sb("scrF", [128, NSF, 512], F32)
    scrB = sb("scrB", [128, NSB, 640], BF16)
    stat = sb("stat", [128, 8, 8], F32)
    Pb4 = sb("Pb4", [128, 4, 640], BF16)
    PTs3 = sb("PTs3", [128, 3, 640], BF16)
    otm3 = sb("otm3", [128, 3, 128], BF16)
    nsink = sb("nsink", [128, depth * 8], F32)
    bnst = sb("bnst", [128, 2, 4, 6], F32)
    bnag = sb("bnag", [128, 2, 4, 2], F32)
    pp = [es.enter_context(nc.psum_tensor(f"pp{i}", [128, 2, 512], F32)) for i in range(4)]
    ppb = [t.bitcast(BF16) for t in pp]

    def PS(b):
        return pp[b // 2][:, b % 2, :]

    def PSB(b):
        return ppb[b // 2][:, b % 2, :]

    ctr = dict(bank=0, f=0, b=0, ring=0, st=0, bn=0)

    def nbank():
        b = ctr["bank"]
        ctr["bank"] = (b + 1) % 8
        return b

    def allocF():
        i = ctr["f"]
        ctr["f"] = (i + 1) % NSF
        return ("scrF", i), scrF[:, i, :]

    def allocB():
        i = ctr["b"]
        ctr["b"] = (i + 1) % NSB
        return ("scrB", i), scrB[:, i, :]

    identb = tabb[:, O_ID:O_ID + 128]
    permb = tabb[:, O_PERM:O_PERM + 128]
    onesb = tabb[:, O_ONES:O_ONES + 128]
    TB = ("tabb",)
    TF = ("tabf",)

    def load_unit(dram2d, ncols, after=()):
        ns = ncols // 2048
        if ctr["ring"] + ns > RING:
            ctr["ring"] = 0
        s0 = ctr["ring"]
        ctr["ring"] += ns
        keys = [("ring", s0 + i) for i in range(ns)]
        for a in range(0, ns, 2):
            n = min(2, ns - a)
            dst = ring[:, (s0 + a) * 2048:(s0 + a + n) * 2048]
            src = dram2d[:, a * 2048:(a + n) * 2048]
            p.dma("pool", f"ring{s0 + a}", list(after), keys[a:a + n],
                  lambda e, dst=dst, src=src: e.dma_start(out=dst, in_=src))
        return ring[:, s0 * 2048:(s0 + ns) * 2048], keys

    def pe_mms(reads, writes, mms):
        def fn(e, mms=mms):
            ins = None
            for (o, lh, rh, st, sp) in mms:
                ins = e.matmul(o, lhsT=lh, rhs=rh, start=st, stop=sp)
            return ins
        p.op("pe", reads, writes, fn)

    def act(reads, writes, out, in_, func, bias=None, scale=None, accum_out=None):
        kw = {}
        if bias is not None:
            kw["bias"] = bias
        if scale is not None:
            kw["scale"] = scale
        if accum_out is not None:
            kw["accum_out"] = accum_out
        p.op("act", reads, writes,
             lambda e: e.activation(out=out, in_=in_, func=func, **kw))

    def dve_ts(reads, writes, out, in0, s1, s2, op0, op1=None):
        if op1 is None:
            p.op("dve", reads, writes,
                 lambda e: e.tensor_scalar(out=out, in0=in0, scalar1=s1, scalar2=None, op0=op0))
        else:
            p.op("dve", reads, writes,
                 lambda e: e.tensor_scalar(out=out, in0=in0, scalar1=s1, scalar2=s2, op0=op0, op1=op1))

    def dve_tt(reads, writes, out, in0, in1, op):
        p.op("dve", reads, writes, lambda e: e.tensor_tensor(out=out, in0=in0, in1=in1, op=op))

    def dve_stt(reads, writes, out, in0, scalar, in1, op0, op1):
        p.op("dve", reads, writes,
             lambda e: e.scalar_tensor_tensor(out=out, in0=in0, scalar=scalar, in1=in1, op0=op0, op1=op1))

    def xk(dc, s):
        return ("xg", dc, s)

    def hk(dc, s):
        return ("hm", dc, s)

    p.dma("sp", "tabf", [], [TF], lambda e: e.dma_start(out=tabf[:, :], in_=DR["tabf"]))
    p.dma("sp", "tabb", [], [TB], lambda e: e.dma_start(out=tabb[:, :], in_=DR["tabb"]))
    G0 = depth * LF
    p.op("dve", [], [("k", 0, 0), ("k", 1, 0)], lambda e: e.memset(kT[:, :, 0:128], 0.0))
    p.op("dve", [], [("v", 0)], lambda e: e.memset(v_tm[:, 0, :], 0.0))
    p.op("dve", [], [("epsD",)], lambda e: e.memset(epsD[:, 0:1], D * EPS))
    p.op("dve", [("epsD",)], [("epsD",)], lambda e: e.memset(epsD[:, 1:2], EPS))
    act([TF], [("sT",)], sT[:, :, 0], tabf[:, G0 + 16:G0 + 32], AF.Silu)
    act([TF, ("sT",)], [("sT",)], sT[:, :, 1], tabf[:, G0 + 32:G0 + 48], AF.Silu)
    dve_ts([TF], [("fgs",)], fgs[:, :], tabf[:, G0:G0 + 16], SQD, None, ALU.mult)
    for l_ in range(depth):
        dve_ts([TF], [("nsink",)], nsink[:, l_ * 8:(l_ + 1) * 8], tabf[:, l_ * LF + 136:l_ * LF + 144],
               -1.0, None, ALU.mult)

    p.stage("tables")

    def ada_chunk(l, j0, j1, bank=None):
        if bank is None:
            bank = nbank()
        nj = j1 - j0
        for j in range(j0, j1):
            u, keys = load_unit(DR["w_adab"][l * 96 + j], 2048)
            o = 2 * (j - j0)
            mms = [(PS(bank)[:, o:o + 2], u[:, dc * 128:(dc + 1) * 128], sT[:, dc, :], dc == 0, dc == 15)
                   for dc in range(16)]
            pe_mms(keys + [("sT",)], [("ps", bank)], mms)
        base = l * LF
        psv = PS(bank)[:, 0:2 * nj].rearrange("p (j r) -> p j r", r=2)
        for r in range(2):
            dve_tt([("ps", bank), TF], [("modc", l, r, j0)], modT[:, l, j0:j1, r], psv[:, :, r],
                   tabf[:, base + 32 + j0:base + 32 + j1], ALU.add)
        return [("modc", l, r, j0) for r in range(2)]

    def ada_finish(l, ckeys):
        base = l * LF
        for r in range(2):
            p.op("dve", [k for k in ckeys if k[2] == r], [("mod", l, r)], lambda e: e.engine_nop())
            for w, (goff, koff) in enumerate(((0, 16), (16, 64))):
                dve_stt([("mod", l, r), TF], [("av", l, w, r)], av[:, l, w, :, r],
                        modT[:, l, koff:koff + 16, r], 1.0, tabf[:, base + goff:base + goff + 16],
                        ALU.add, ALU.mult)
                dve_ts([("av", l, w, r)], [("av", l, w, r)], av[:, l, w, :, r], av[:, l, w, :, r],
                       SQD, None, ALU.mult)

    ck0 = []
    for j0 in range(0, 96, 32):
        ck0 += ada_chunk(0, j0, j0 + 32)
    ada_finish(0, ck0)
    p.stage("ada")

    def norm_runs(l, runs, which, dst_final=False):
        for (a, b, r, slots) in runs:
            n = b - a
            bank = nbank()
            for dc in range(16):
                kq, sq = allocB()
                act([xk(dc, s) for s in slots], [kq], sq[:, :n], xg[:, dc, a:b], AF.Square)
                pe_mms([kq, TB], [("ps", bank)], [(PS(bank)[:, :n], onesb, sq[:, :n], dc == 0, dc == 15)])
            rk = [("rstd", s) for s in slots]
            act([("ps", bank), ("epsD",)], rk, rstd[:, a:b], PS(bank)[:, :n], AF.Sqrt, bias=epsD[:, 0:1], scale=1.0)
            p.op("dve", rk, rk, lambda e, a=a, b=b: e.reciprocal(out=rstd[:, a:b], in_=rstd[:, a:b]))
            for dc in range(16):
                if dst_final:
                    dve_stt([xk(dc, s) for s in slots] + rk + [("fgs",)], [xk(dc, s) for s in slots],
                            xg[:, dc, a:b], xg[:, dc, a:b], fgs[:, dc:dc + 1], rstd[:, a:b], ALU.mult, ALU.mult)
                    continue
                kf, xn = allocF()
                dve_tt([xk(dc, s) for s in slots] + rk, [kf], xn[:, :n], xg[:, dc, a:b], rstd[:, a:b], ALU.mult)
                bcol = 0 if which == 0 else 48
                act([kf, ("av", l, which, r), ("mod", l, r)], [hk(dc, s) for s in slots],
                    hm[:, dc, a:b], xn[:, :n], AF.Identity,
                    bias=modT[:, l, bcol + dc, r:r + 1], scale=av[:, l, which, dc, r:r + 1])

    def run_layer(l):
        last = l == depth - 1
        base = l * LF
        xsrc = xin_v if l == 0 else xs_v
        xsrc_key = "xin" if l == 0 else "xs"
        p.dma("pool", "poolw", [], [("poolw",)], lambda e: e.dma_start(out=poolw[:, :], in_=DR["poolw"][l]))
        p.dma("pool", "sguw", [], [("sguw",)], lambda e: e.dma_start(out=sguw[:, :], in_=DR["sguw"][l]))
        computed = set()
        plan = layer_plan(l, depth)
        ng_fc = sum(1 for g_ in plan if g_["fc"])
        ada_state = dict(j=0, keys=[], gi=0)

        def ada_fill(target, bank=None, maxn=96):
            if l + 1 >= depth:
                return
            target = min(96, target, ada_state["j"] + maxn)
            while ada_state["j"] < target:
                j0 = ada_state["j"]
                j1 = min(target, j0 + 4)
                ada_state["keys"] += ada_chunk(l + 1, j0, j1, bank)
                ada_state["j"] = j1
        for grp in plan:
            fc, kvo = grp["fc"], grp["kv"]
            slots = fc + kvo
            ns, nfc = len(slots), len(fc)

            def col(blk):
                return blk[1] * 128 if blk[0] == "C" else 256 + blk[1] * 128

            def make_runs(sl_idx):
                runs = []
                for s in sl_idx:
                    blk = slots[s]
                    if runs and runs[-1][2] == blk[0] and runs[-1][1] == s and \
                            slots[s - 1][1] + 1 == blk[1] and runs[-1][1] - runs[-1][0] < 4:
                        runs[-1][1] = s + 1
                    else:
                        runs.append([s, s + 1, blk[0]])
                return runs

            all_runs = make_runs(range(ns))
            if nfc:
                gi = ada_state["gi"]
                ada_state["gi"] = gi + 1
                q0 = -(-96 * gi // ng_fc)
                q1 = -(-96 * (gi + 1) // ng_fc)
            for (s0, s1, kind) in all_runs:
                c0 = col(slots[s0])
                n = (s1 - s0) * 128
                for q4 in range(4):
                    keys = [xk(dc, s) for dc in range(4 * q4, 4 * q4 + 4) for s in range(s0, s1)]
                    rk = [(xsrc_key, slots[s], q4) for s in range(s0, s1)]
                    p.dma("sp", f"xg{s0}_{q4}", rk, keys,
                          lambda e, s0=s0, n=n, c0=c0, q4=q4: e.dma_start(
                              out=xg[:, 4 * q4:4 * q4 + 4, s0 * 128:s0 * 128 + n],
                              in_=xsrc[:, 4 * q4:4 * q4 + 4, c0:c0 + n]))
            nr = [(s0 * 128, s1 * 128, 1 if kind == "C" else 0, list(range(s0, s1))) for (s0, s1, kind) in all_runs]
            p.stage("xload")
            norm_runs(l, nr, 0)
            p.stage("norm1")

            need = [s for s in range(ns) if slots[s] not in computed]
            need_runs = make_runs(need)

            def hkeys(s0, s1):
                return [hk(dc, s) for dc in range(16) for s in range(s0, s1)]

            for kh in range(2):
                aft = [xk(dc, s) for dc in (3, 7, 11, 15) for s in range(ns)] if kh == 0 else ()
                u, ukeys = load_unit(DR["w_inb"][l * 14 + 8 + kh], 2048, after=aft)
                for (s0, s1, kind) in need_runs:
                    n = (s1 - s0) * 128
                    a = s0 * 128
                    bank = nbank()
                    if kh == 0:
                        for dc in range(16):
                            pe_mms(ukeys + [hk(dc, s) for s in range(s0, s1)], [("ps", bank)],
                                   [(PS(bank)[:, :n], u[:, dc * 128:(dc + 1) * 128], hm[:, dc, a:a + n],
                                     dc == 0, dc == 15)])
                    else:
                        pe_mms(ukeys + hkeys(s0, s1), [("ps", bank)],
                               [(PS(bank)[:, :n], u[:, dc * 128:(dc + 1) * 128], hm[:, dc, a:a + n], dc == 0, dc == 15)
                                for dc in range(16)])
                    if kind == "C":
                        c0 = slots[s0][1] * 128
                        act([("ps", bank)], [("kc", kh)], kcT[:, kh, c0:c0 + n], PS(bank)[:, :n], AF.Copy)
                    else:
                        li = slots[s0][1]
                        dst = kT[:, kh, (li + 1) * 128:(li + 1) * 128 + n]
                        dkeys = [("k", kh, li + 1 + i) for i in range(s1 - s0)]
                        rope_evac(bank, n, li, dst, dkeys)
            p.stage("K")
            u, ukeys = load_unit(DR["w_inv"][l], 4096)
            for s in need:
                blk = slots[s]
                vs = NLAT + 1 + blk[1] if blk[0] == "C" else blk[1] + 1
                bank = nbank()
                pe_mms(ukeys + hkeys(s, s + 1), [("ps", bank)],
                       [(PS(bank)[:, :256], hm[:, dc, s * 128:(s + 1) * 128], u[:, dc * 256:(dc + 1) * 256],
                         dc == 0, dc == 15) for dc in range(16)])
                act([("ps", bank)], [("v", vs)], v_tm[:, vs, :], PS(bank)[:, :256], AF.Copy)
            p.stage("V")
            if not (last and nfc == 0):
                u, ukeys = load_unit(DR["w_inp"][l], 8192)
                for s in need:
                    blk = slots[s]
                    ps_ = blk[1] if blk[0] == "C" else 2 + blk[1]
                    bank = nbank()
                    pe_mms(ukeys + hkeys(s, s + 1), [("ps", bank)],
                           [(PS(bank)[:, :], hm[:, dc, s * 128:(s + 1) * 128], u[:, dc * 512:(dc + 1) * 512],
                             dc == 0, dc == 15) for dc in range(16)])
                    p.op("dve", [("ps", bank)], [("p", ps_)],
                         lambda e, ps_=ps_, bank=bank: e.tensor_copy(out=p_tm[:, ps_, :], in_=PS(bank)[:, :]))
            p.stage("P")
            for s in need:
                computed.add(slots[s])
            if nfc == 0:
                continue
            nf = nfc * 128
            fc_runs = make_runs(range(nfc))
            for h in range(8):
                u, ukeys = load_unit(DR["w_inb"][l * 14 + h], 2048)
                bank = nbank()
                pe_mms(ukeys + hkeys(0, nfc), [("ps", bank)],
                       [(PS(bank)[:, :nf], u[:, dc * 128:(dc + 1) * 128], hm[:, dc, 0:nf], dc == 0, dc == 15)
                        for dc in range(16)])
                for (s0, s1, kind) in fc_runs:
                    n = (s1 - s0) * 128
                    a = s0 * 128
                    dkeys = [("q", h, s) for s in range(s0, s1)]
                    if kind == "C":
                        act([("ps", bank)], dkeys, qT[:, h, a:a + n], PS(bank)[:, a:a + n], AF.Copy)
                    else:
                        rope_evac(bank, n, slots[s0][1], qT[:, h, a:a + n], dkeys, poff=a)
            p.stage("Q")
            for g in range(4):
                u, ukeys = load_unit(DR["w_inb"][l * 14 + 10 + g], 2048)
                bank = nbank()
                pe_mms(ukeys + hkeys(0, nfc), [("ps", bank)],
                       [(PS(bank)[:, :nf], u[:, dc * 128:(dc + 1) * 128], hm[:, dc, 0:nf], dc == 0, dc == 15)
                        for dc in range(16)])
                act([("ps", bank)], [("u", g, s) for s in range(nfc)], uT[:, g, 0:nf], PS(bank)[:, :nf], AF.Gelu)
            p.stage("U")
            u, ukeys = load_unit(DR["w_inz"][l], 8192)
            for s in range(nfc):
                bank = nbank()
                pe_mms(ukeys + hkeys(s, s + 1), [("ps", bank)],
                       [(PS(bank)[:, :], hm[:, dc, s * 128:(s + 1) * 128], u[:, dc * 512:(dc + 1) * 512],
                         dc == 0, dc == 15) for dc in range(16)])
                kf, gz = allocF()
                act([("ps", bank)], [kf], gz[:, :], PS(bank)[:, :], AF.Gelu)
                bi = ctr["bn"]
                ctr["bn"] = (bi + 1) % 2
                for g in range(4):
                    p.op("dve", [kf], [("bnst", bi, g)],
                         lambda e, g=g, bi=bi, gz=gz: e.bn_stats(out=bnst[:, bi, g, :], in_=gz[:, g * 128:(g + 1) * 128]))
                    p.op("dve", [("bnst", bi, g)], [("bnag", bi, g)],
                         lambda e, g=g, bi=bi: e.bn_aggr(out=bnag[:, bi, g, :], in_=bnst[:, bi, g, :]))
                bk = [("bnag", bi, g) for g in range(4)]
                act(bk + [("epsD",)], bk, bnag[:, bi, :, 1], bnag[:, bi, :, 1], AF.Sqrt, bias=epsD[:, 1:2], scale=1.0)
                p.op("dve", bk, bk, lambda e, bi=bi: e.reciprocal(out=bnag[:, bi, :, 1], in_=bnag[:, bi, :, 1]))
                for g in range(4):
                    dve_ts([kf, ("bnag", bi, g)], [("zn", s, g)], zn[:, s, g * 128:(g + 1) * 128],
                           gz[:, g * 128:(g + 1) * 128], bnag[:, bi, g, 0:1], bnag[:, bi, g, 1:2],
                           ALU.subtract, ALU.mult)

            p.stage("Z")
            pool_st = []
            for s in range(nfc):
                blk = slots[s]
                if blk[0] == "C":
                    me = blk[1]
                    srcs = [(0, 4)] if me == 1 else []
                    srcs += [(me, 6 + me)]
                    srcs += [(1, 5)] if me == 0 else []
                else:
                    li = blk[1]
                    srcs = [(2 + li - 1, 0)] if li > 0 else []
                    srcs += [(2 + li, 3 if li == 0 else 1)]
                    srcs += [(2 + li + 1, 2)]
                bank = nbank()
                mms = []
                for g in range(4):
                    for j, (pslot, kind) in enumerate(srcs):
                        mo = O_PM + (g * 8 + kind) * 128
                        mms.append((PS(bank)[:, g * 128:(g + 1) * 128], p_tm[:, pslot, g * 128:(g + 1) * 128],
                                    tabb[:, mo:mo + 128], j == 0, j == len(srcs) - 1))
                pe_mms([("p", ps_) for ps_, _ in srcs] + [TB], [("ps", bank)], mms)
                kb, pl = ("Pb", s), Pb4[:, s, :]
                act([("ps", bank)], [kb], pl[:, 0:512], PS(bank)[:, :], AF.Copy)
                pool_st.append((kb, pl))
            sgu_st = []
            for s in range(nfc):
                bank = nbank()
                pe_mms([("zn", s, g) for g in range(4)] + [("sguw",)], [("ps", bank)],
                       [(PS(bank)[:, g * 128:(g + 1) * 128], zn[:, s, g * 128:(g + 1) * 128],
                         sguw[:, g * 128:(g + 1) * 128], True, True) for g in range(4)])
                sgu_st.append(bank)
            for s in range(nfc):
                bank = sgu_st[s]
                kf, tmp = allocF()
                for g in range(4):
                    dve_stt([("ps", bank), TF], [kf], tmp[:, g * 128:(g + 1) * 128],
                            PS(bank)[:, g * 128:(g + 1) * 128], tabf[:, base + 132 + g:base + 133 + g],
                            tabf[:, base + 144 + g * 128:base + 144 + (g + 1) * 128], ALU.mult, ALU.add)
                for g in range(4):
                    dve_tt([kf, ("u", g, s)], [hk(12 + g, s)], hm[:, 12 + g, s * 128:(s + 1) * 128],
                           tmp[:, g * 128:(g + 1) * 128], uT[:, g, s * 128:(s + 1) * 128], ALU.mult)
            for s in range(nfc):
                kb, pl = pool_st[s]
                bank2 = nbank()
                pe_mms([kb, ("poolw",)], [("ps", bank2)],
                       [(PS(bank2)[:, g * 128:(g + 1) * 128], poolw[:, g * 128:(g + 1) * 128],
                         pl[:, g * 128:(g + 1) * 128], True, True) for g in range(4)])
                for g in range(4):
                    act([("ps", bank2), TF], [hk(8 + g, s)], hm[:, 8 + g, s * 128:(s + 1) * 128],
                        PS(bank2)[:, g * 128:(g + 1) * 128], AF.Identity,
                        scale=tabf[:, base + 128 + g:base + 129 + g])

            items = [(s, h) for s in range(nfc) for h in range(8)]
            NI = len(items)
            ist = [dict() for _ in items]

            def st_A(i):
                s, h = items[i]
                kh = h // 4
                blk = slots[s]
                pi = i % 2
                qa = qT[:, h, s * 128:(s + 1) * 128]
                wr = [("ps", 2 * pi), ("ps", 2 * pi + 1)]
                if blk[0] == "C":
                    pe_mms([("q", h, s), ("kc", kh)], wr,
                           [(pp[pi][:, 0, 0:256], qa, kcT[:, kh, :], True, True)])
                else:
                    li = blk[1]
                    c0 = li * 128
                    mo = O_MASK + (384 if li == 0 else 0)
                    mk = tabb[:, mo:mo + 384]
                    pe_mms([("q", h, s), ("kc", kh), TB] + [("k", kh, li + i2) for i2 in range(3)], wr,
                           [(pp[pi][:, 0, 0:320], qa, kT[:, kh, c0:c0 + 320], True, False),
                            (pp[pi][:, 0, 0:320], identb, mk[:, 0:320], False, True),
                            (pp[pi][:, 1, 0:64], qa, kT[:, kh, c0 + 320:c0 + 384], True, False),
                            (pp[pi][:, 1, 0:64], identb, mk[:, 320:384], False, True),
                            (pp[pi][:, 1, 64:320], qa, kcT[:, kh, :], True, True)])

            def st_B(i):
                s, h = items[i]
                blk = slots[s]
                pi = i % 2
                isC = blk[0] == "C"
                rd = [("ps", 2 * pi), ("ps", 2 * pi + 1)]
                si = ctr["st"]
                ctr["st"] = (si + 1) % 8
                sk = ("stat", si)
                d = ist[i]
                d["si"] = si
                if isC:
                    sv = pp[pi][:, 0, 0:256]
                    nk = 256
                    p.op("dve", rd, [sk], lambda e: e.reduce_max(out=stat[:, si, 0:1], in_=sv, axis=AX.X))
                else:
                    sv = pp[pi][:, :, 0:320]
                    nk = 640
                    p.op("dve", rd, [sk], lambda e: e.reduce_max(out=stat[:, si, 0:1], in_=sv, axis=AX.XY))
                d["nk"] = nk
                dve_ts([sk], [sk], stat[:, si, 2:3], stat[:, si, 0:1], -SCALE, None, ALU.mult)
                pbi = i % 4
                kb, Pb = ("Pb", pbi), Pb4[:, pbi, :]
                d["kb"], d["Pb"] = kb, Pb
                if isC:
                    pout = Pb[:, 0:256]
                else:
                    pout = Pb[:, 0:640].rearrange("p (a b) -> p a b", a=2)
                sink_ap = tabf[:, base + 136 + h:base + 137 + h]
                act(rd + [sk], [kb, ("stat_rs", si)], pout, sv, AF.Exp, bias=stat[:, si, 2:3], scale=SCALE,
                    accum_out=stat[:, si, 3:4])
                act([sk, TF], [("stat_es", si)], stat[:, si, 4:5], stat[:, si, 2:3], AF.Exp, bias=sink_ap, scale=1.0)

            def st_C(i):
                d = ist[i]
                tb = 4
                nb = d["nk"] // 128
                Pb = d["Pb"]
                pe_trs([d["kb"], TB], [("ps", tb)],
                       [(PSB(tb)[:, j * 128:(j + 1) * 128], Pb[:, j * 128:(j + 1) * 128]) for j in range(nb)])
                pti = i % 3
                kb2, PTs = ("PTs", pti), PTs3[:, pti, :]
                d["kb2"], d["PTs"] = kb2, PTs
                act([("ps", tb)], [kb2], PTs[:, :d["nk"]], PSB(tb)[:, :d["nk"]], AF.Copy)

            def st_E(i):
                s, h = items[i]
                kh = h // 4
                blk = slots[s]
                d = ist[i]
                si = d["si"]
                nb = d["nk"] // 128
                PTs = d["PTs"]
                if blk[0] == "C":
                    vsl = [NLAT + 1, NLAT + 2]
                else:
                    li = blk[1]
                    vsl = [li, li + 1, li + 2, NLAT + 1, NLAT + 2]
                ob = 6 + i % 2
                pe_mms([d["kb2"]] + [("v", v) for v in vsl], [("ps", ob)],
                       [(PS(ob)[:, 0:128], PTs[:, j * 128:(j + 1) * 128],
                         v_tm[:, vsl[j], kh * 128:(kh + 1) * 128], j == 0, j == nb - 1) for j in range(nb)])
                dve_tt([("stat_rs", si), ("stat_es", si)], [("stat_d", si)], stat[:, si, 5:6], stat[:, si, 3:4],
                       stat[:, si, 4:5], ALU.add)
                p.op("dve", [("stat_d", si)], [("stat_r", si)],
                     lambda e: e.reciprocal(out=stat[:, si, 6:7], in_=stat[:, si, 5:6]))
                oi = i % 3
                kb3, otm = ("otm", oi), otm3[:, oi, :]
                d["kb3"], d["otm"] = kb3, otm
                dve_ts([("ps", ob), ("stat_r", si)], [kb3], otm[:, 0:128], PS(ob)[:, 0:128],
                       stat[:, si, 6:7], None, ALU.mult)

            def st_G(i):
                s, h = items[i]
                d = ist[i]
                ob = 6 + i % 2
                pe_trs([d["kb3"], TB], [("ps", ob)], [(PSB(ob)[:, 512:640], d["otm"][:, 0:128])])
                p.op("dve", [("ps", ob)], [hk(h, s)],
                     lambda e, h=h, s=s, ob=ob: e.tensor_copy(out=hm[:, h, s * 128:(s + 1) * 128],
                                                              in_=PSB(ob)[:, 512:640]))

            for t in range(NI + 4):
                if 0 <= t - 4 < NI:
                    st_G(t - 4)
                if 0 <= t - 3 < NI:
                    st_E(t - 3)
                if 0 <= t - 2 < NI:
                    st_C(t - 2)
                if t % 4 == 1:
                    ada_fill(q1, bank=5, maxn=3)
                if t < NI:
                    st_A(t)
                    st_B(t)

            p.stage("attn")
            p.stage("sgu")
            def resid(bank, j, gate_k):
                for (s0, s1, kind) in fc_runs:
                    r = 1 if kind == "C" else 0
                    a, b = s0 * 128, s1 * 128
                    keys = [xk(j, s) for s in range(s0, s1)]
                    dve_stt([("ps", bank), ("mod", l, r)] + keys, keys, xg[:, j, a:b], PS(bank)[:, a:b],
                            modT[:, l, gate_k + j, r:r + 1], xg[:, j, a:b], ALU.mult, ALU.add)

            for j in range(16):
                u, ukeys = load_unit(DR["w_outb"][l * 16 + j], 2048)
                bank = nbank()
                pe_mms(ukeys + hkeys(0, nfc), [("ps", bank)],
                       [(PS(bank)[:, :nf], u[:, mc * 128:(mc + 1) * 128], hm[:, mc, 0:nf], mc == 0, mc == 15)
                        for mc in range(16)])
                resid(bank, j, 32)

            p.stage("outproj")
            nr2 = [(s0 * 128, s1 * 128, 1 if kind == "C" else 0, list(range(s0, s1))) for (s0, s1, kind) in fc_runs]
            norm_runs(l, nr2, 1)
            ada_fill(-(-96 * ada_state["gi"] // ng_fc))

            p.stage("norm2")
            def ffn_gu(G, c):
                hb = G % 2
                ci = G * 4 + c
                ug, gkeys = load_unit(DR["w_gub"][l * 88 + 2 * ci], 2048)
                uu, ukeys2 = load_unit(DR["w_gub"][l * 88 + 2 * ci + 1], 2048)
                bg, bu = nbank(), nbank()
                if ci == 0:
                    for dc in range(16):
                        pe_mms(gkeys + [hk(dc, s) for s in range(nfc)], [("ps", bg)],
                               [(PS(bg)[:, :nf], ug[:, dc * 128:(dc + 1) * 128], hm[:, dc, 0:nf],
                                 dc == 0, dc == 15)])
                else:
                    pe_mms(gkeys + hkeys(0, nfc), [("ps", bg)],
                           [(PS(bg)[:, :nf], ug[:, dc * 128:(dc + 1) * 128], hm[:, dc, 0:nf], dc == 0, dc == 15)
                            for dc in range(16)])
                pe_mms(ukeys2 + hkeys(0, nfc), [("ps", bu)],
                       [(PS(bu)[:, :nf], uu[:, dc * 128:(dc + 1) * 128], hm[:, dc, 0:nf], dc == 0, dc == 15)
                        for dc in range(16)])
                kf, sg = allocF()
                act([("ps", bg)], [kf], sg[:, :nf], PS(bg)[:, :nf], AF.Silu)
                dve_tt([kf, ("ps", bu)], [("hT", hb, c)], hT[:, hb, c, 0:nf], sg[:, :nf], PS(bu)[:, :nf], ALU.mult)

            def ffn_wd(G):
                return [load_unit(DR["w_dn"][l * 44 + G * 4 + c], 2048) for c in range(4)]

            def ffn_down(G, wd, j0, j1):
                hb = G % 2
                for j in range(j0, j1):
                    bank = nbank()
                    rd = [("hT", hb, c) for c in range(4)]
                    for c in range(4):
                        rd += wd[c][1]
                    pe_mms(rd, [("ps", bank)],
                           [(PS(bank)[:, :nf], wd[c][0][:, j * 128:(j + 1) * 128], hT[:, hb, c, 0:nf], c == 0, c == 3)
                            for c in range(4)])
                    resid(bank, j, 80)

            for G in range(11):
                wd_prev = None
                for c in range(4):
                    ffn_gu(G, c)
                    if G >= 1 and c == 0:
                        ffn_down(G - 1, ffn_wd(G - 1), 0, 16)
            ffn_down(10, ffn_wd(10), 0, 16)

            p.stage("ffn")
            if not last:
                for (s0, s1, kind) in fc_runs:
                    c0 = col(slots[s0])
                    n = (s1 - s0) * 128
                    for q4 in range(4):
                        keys = [xk(dc, s) for dc in range(4 * q4, 4 * q4 + 4) for s in range(s0, s1)]
                        wk = [("xs", slots[s], q4) for s in range(s0, s1)]
                        p.dma("sp", f"xst{s0}_{q4}", keys, wk,
                              lambda e, s0=s0, n=n, c0=c0, q4=q4: e.dma_start(
                                  out=xs_v[:, 4 * q4:4 * q4 + 4, c0:c0 + n],
                                  in_=xg[:, 4 * q4:4 * q4 + 4, s0 * 128:s0 * 128 + n]))
            else:
                norm_runs(l, nr2, 0, dst_final=True)
                for (s0, s1, kind) in fc_runs:
                    c0 = slots[s0][1] * 128
                    n = (s1 - s0) * 128
                    keys = [xk(dc, s) for dc in range(16) for s in range(s0, s1)]
                    p.dma("sp", f"xst{s0}", keys, [("out", slots[s]) for s in range(s0, s1)],
                          lambda e, s0=s0, n=n, c0=c0: e.dma_start(out=out_v[:, :, c0:c0 + n],
                                                                   in_=xg[:, :, s0 * 128:s0 * 128 + n]))
        if l + 1 < depth:
            assert ada_state["j"] == 96
            ada_finish(l + 1, ada_state["keys"])

    def pe_trs(reads, writes, trs):
        def fn(e, trs=trs):
            ins = None
            for (o, i_) in trs:
                ins = e.transpose(out=o, in_=i_, identity=identb)
            return ins
        p.op("pe", reads, writes, fn)

    def rope_evac(bank, n, li, dst, dkeys, poff=0):
        src = PS(bank)[:, poff:poff + n]
        kb, kraw = allocB()
        act([("ps", bank)], [kb], kraw[:, :n], src, AF.Copy)
        kf1, t1 = allocF()
        dve_tt([("ps", bank), TB], [kf1], t1[:, :n], src, tabb[:, O_COS + li * 128:O_COS + li * 128 + n], ALU.mult)
        b2 = nbank()
        pe_mms([kb, TB], [("ps", b2)], [(PS(b2)[:, :n], permb, kraw[:, :n], True, True)])
        kf2, t2 = allocF()
        dve_tt([("ps", b2), TB], [kf2], t2[:, :n], PS(b2)[:, :n], tabb[:, O_SIN + li * 128:O_SIN + li * 128 + n], ALU.mult)
        dve_tt([kf1, kf2], dkeys, dst, t1[:, :n], t2[:, :n], ALU.add)

    for l in range(depth):
        run_layer(l)

    semnames = ["pe", "act", "dve"] + sorted(p.dcnt.keys())
    SEM = {n_: es.enter_context(nc.semaphore(f"s_{n_}")) for n_ in semnames}
    finals = [(k, v) for k, v in p.dcnt.items() if k.startswith("xst")]

    def replay(eng, e):
        for waits, fn, (sk, inc) in p.ops[eng]:
            for k, v in waits:
                e.wait_ge(SEM[k], v)
            fn(e).then_inc(SEM[sk], inc)
        if eng == "sp":
            for k, v in finals:
                e.wait_ge(SEM[k], v)

    with es:
        with nc.Block() as block:
            @block.tensor
            def _(e):
                replay("pe", e)

            @block.scalar
            def _(e):
                replay("act", e)

            @block.vector
            def _(e):
                replay("dve", e)

            @block.gpsimd
            def _(e):
                replay("pool", e)

            @block.sync
            def _(e):
                replay("sp", e)
    return nc


def _unitize(W):
    nj = W.shape[1] // 128
    return np.ascontiguousarray(W.reshape(16, 128, nj, 128).transpose(2, 1, 0, 3)).reshape(nj, 128, 2048)


def _rowblock(W, c0, c1):
    C = c1 - c0
    return np.ascontiguousarray(W[:, c0:c1].reshape(16, 128, C).transpose(1, 0, 2)).reshape(128, 16 * C)


def _pp(v, n):
    return np.ascontiguousarray(np.asarray(v, np.float32).reshape(n, 128).T)


def _pool_mats(mirror, S=2048):
    out = np.zeros((4, 8, 128, 128), np.float32)
    for g, w in enumerate((2, 4, 8, 16)):
        half = w // 2
        idx = np.arange(384)
        gl = (S - 1 - idx) if mirror else idx

        def dense(gpos, Sq):
            lo = np.clip(gpos - half, 0, Sq)
            hi = np.clip(gpos + half, 0, Sq)
            cnt = (hi - lo).astype(np.float32)
            inw = (gpos[:, None] >= lo[None, :]) & (gpos[:, None] < hi[None, :])
            M = np.where(inw, (np.float32(1.0) / cnt)[None, :], np.float32(0.0)).astype(np.float32)
            M = M - np.eye(len(gpos), dtype=np.float32)
            return M

        M = dense(gl, S)
        out[g, 0] = M[0:128, 128:256]
        out[g, 1] = M[128:256, 128:256]
        out[g, 2] = M[256:384, 128:256]
        out[g, 3] = M[0:128, 0:128]
        Mc = dense((255 - np.arange(256)) if mirror else np.arange(256), 256)
        out[g, 4] = Mc[0:128, 128:256]
        out[g, 5] = Mc[128:256, 0:128]
        out[g, 6] = Mc[0:128, 0:128]
        out[g, 7] = Mc[128:256, 128:256]
    return out


def _host_prep(inputs, depth):
    f32 = np.float32
    NLAT = 8 + depth
    g = {k: np.asarray(v) for k, v in inputs.items()}
    shared = {}
    w_in = g["w_in"][:depth]
    shared["w_inb"] = np.concatenate(
        [np.concatenate([_unitize(w_in[l][:, 0:1280]), _unitize(w_in[l][:, 2048:2560])], 0) for l in range(depth)], 0)
    shared["w_inv"] = np.stack([_rowblock(w_in[l], 1280, 1536) for l in range(depth)])
    shared["w_inp"] = np.stack([_rowblock(w_in[l], 1536, 2048) for l in range(depth)])
    shared["w_inz"] = np.stack([_rowblock(w_in[l], 2560, 3072) for l in range(depth)])
    shared["w_outb"] = np.concatenate([_unitize(g["w_out"][l]) for l in range(depth)], 0)
    gub = []
    for l in range(depth):
        wg = _unitize(g["w_gate_up"][l][:, :5632])
        wu = _unitize(g["w_gate_up"][l][:, 5632:])
        gub.append(np.stack([wg, wu], 1).reshape(88, 128, 2048))
    shared["w_gub"] = np.concatenate(gub, 0)
    shared["w_dn"] = np.ascontiguousarray(g["w_down"][:depth].reshape(depth * 44, 128, 2048))
    shared["w_adab"] = np.concatenate([_unitize(g["w_ada"][l]) for l in range(depth)], 0)
    shared["poolw"] = np.ascontiguousarray(g["pool_w"][:depth].transpose(0, 2, 1, 3)).reshape(depth, 128, 512)
    sguw_n = np.ascontiguousarray(g["sgu_w"][:depth].transpose(0, 3, 1, 2)).reshape(depth, 128, 512)
    sguw_m = np.ascontiguousarray(g["sgu_w"][:depth][:, :, ::-1, ::-1].transpose(0, 3, 1, 2)).reshape(depth, 128, 512)

    tab_common = np.zeros((128, depth * LF + 48), f32)
    for l in range(depth):
        b0 = l * LF
        tab_common[:, b0:b0 + 16] = _pp(g["norm_mix_g"][l], 16)
        tab_common[:, b0 + 16:b0 + 32] = _pp(g["norm_ffn_g"][l], 16)
        tab_common[:, b0 + 32:b0 + 128] = _pp(g["b_ada"][l], 96)
        tab_common[:, b0 + 128:b0 + 132] = _pp(g["pool_scale"][l], 4)
        tab_common[:, b0 + 132:b0 + 136] = np.ascontiguousarray(g["sgu_norm_g"][l].T)
        tab_common[:, b0 + 136:b0 + 144] = np.broadcast_to(g["attn_sink"][l][None, :], (128, 8))
        tab_common[:, b0 + 144:b0 + 656] = np.broadcast_to(g["sgu_b"][l].reshape(1, 512), (128, 512))
    G0 = depth * LF
    tab_common[:, G0:G0 + 16] = _pp(g["final_norm_g"], 16)
    tab_common[:, G0 + 32:G0 + 48] = _pp(g["c_ctx"], 16)

    inv_freq = (10000.0 ** (-np.arange(0, 64, 2, dtype=f32) / f32(64))).astype(f32)
    d = np.arange(128)
    axis, half, fr = d // 64, (d % 64) // 32, d % 32
    qi = np.arange(128)[:, None]
    kj = np.arange(128)[None, :]
    m_prev = np.where(kj >= qi, 0.0, NEG).astype(f32)
    m_next = np.where(kj <= qi, 0.0, NEG).astype(f32)
    zero = np.zeros((128, 128), f32)
    mask = np.concatenate([m_prev, zero, m_next, np.full((128, 128), NEG, f32), zero, m_next], 1)
    ident = np.eye(128, dtype=f32)
    partner = np.where((d % 64) < 32, d + 32, d - 32)
    perm = np.zeros((128, 128), f32)
    perm[partner, d] = 1.0
    ones = np.ones((128, 128), f32)

    in_maps = []
    for b in range(4):
        for h in range(2):
            loc = np.arange(NLAT * 128)
            gidx = loc if h == 0 else 2047 - loc
            xt = np.empty((2048, 256 + NLAT * 128), f32)
            xt[:, :256] = (g["ctx"][b] if h == 0 else g["ctx"][b][::-1]).T
            xt[:, 256:] = g["x"][b][gidx].T
            tabf = tab_common.copy()
            tabf[:, G0 + 16:G0 + 32] = _pp(g["c"][b], 16)
            if h == 1:
                for l in range(depth):
                    b0 = l * LF
                    tabf[:, b0 + 144:b0 + 656] = np.broadcast_to(g["sgu_b"][l][:, ::-1].reshape(1, 512), (128, 512))
            row = (gidx // 64).astype(f32)
            colp = (gidx % 64).astype(f32)
            pos = np.where(axis[:, None] == 0, row[None, :], colp[None, :]).astype(f32)
            ang = (pos * inv_freq[fr][:, None]).astype(f32)
            cos_t = np.cos(ang).astype(f32)
            sin_t = np.sin(ang).astype(f32) * np.where(half == 0, -1.0, 1.0).astype(f32)[:, None]
            pm = _pool_mats(h == 1).reshape(32 * 128, 128).reshape(32, 128, 128).transpose(1, 0, 2).reshape(128, 32 * 128)
            tabb = np.concatenate([cos_t, sin_t, pm, mask, ident, perm, ones], 1).astype(ml_dtypes.bfloat16)
            m = dict(shared)
            m["xin"] = xt
            m["sguw"] = sguw_n if h == 0 else sguw_m
            m["tabf"] = tabf
            m["tabb"] = tabb
            in_maps.append(m)
    return in_maps


_NC_CACHE = {}


def kernel(**inputs):
    depth = DEPTH
    in_maps = _host_prep(inputs, depth)
    if depth not in _NC_CACHE:
        _NC_CACHE[depth] = build(depth)
    nc = _NC_CACHE[depth]
    res = run_bass_kernel_spmd(nc, in_maps, core_ids=list(range(8)))
    y = np.empty((4, 2048, 2048), np.float32)
    for b in range(4):
        for h in range(2):
            o = np.asarray(res.results[b * 2 + h]["out"]).T
            if h == 0:
                y[b, 0:1024] = o
            else:
                y[b, 1024:2048] = o[::-1]
    return y
```
